# Optimizing a Trainium2 kernel written in Bass

```python
import jax, jax.numpy as jnp
from jax import lax
import numpy as np

D_MODEL = 1024
BATCH = 32
SEQ = 2048
DEPTH = 2
DEC_BATCH = 8
DEC_SEQ = 4096
PAST_LEN = 128

RET_HEADS = 4
RET_DK = 128
RET_DV = 128
RET_QK_WIDTH = 512
RET_WIDTH = 512
RET_CHUNK = 128
RET_LOG2_FWD = (-5.0, -6.0, -7.0, -8.0)
RET_LOG2_BWD = (-5.5, -6.5, -7.5, -8.5)
MLA_HEADS = 4
MLA_NOPE = 128
MLA_ROPE = 64
MLA_QK = 192
MLA_V = 128
MLA_WIDTH = 512
Q_LORA = 256
KV_LORA = 128
Q_BLOCK = 128
ROPE_BASE = 10000.0
NORM_EPS = 1e-6
IN_SPLITS = (RET_QK_WIDTH, RET_QK_WIDTH, RET_WIDTH, RET_WIDTH, Q_LORA, KV_LORA, MLA_ROPE, MLA_WIDTH, D_MODEL, D_MODEL)
IN_WIDTH = 2 * RET_QK_WIDTH + 2 * RET_WIDTH + Q_LORA + KV_LORA + MLA_ROPE + MLA_WIDTH + 2 * D_MODEL

kernel_name = "hybrid_retention_mla_encoder"


def rms_norm(x, g):
    xf = x.astype(jnp.float32)
    y = xf * lax.rsqrt(jnp.mean(xf * xf, axis=-1, keepdims=True) + NORM_EPS)
    return (y * g.astype(jnp.float32)).astype(x.dtype)


def rope(x):
    S, d = x.shape[1], x.shape[-1]
    inv = ROPE_BASE ** (-jnp.arange(0, d, 2, dtype=jnp.float32) / d)
    ang = jnp.arange(S, dtype=jnp.float32)[:, None] * inv[None, :]
    cos = jnp.cos(ang)[None, :, None, :]
    sin = jnp.sin(ang)[None, :, None, :]
    xf = x.astype(jnp.float32)
    x1, x2 = xf[..., : d // 2], xf[..., d // 2:]
    return jnp.concatenate([x1 * cos - x2 * sin, x1 * sin + x2 * cos], axis=-1).astype(x.dtype)


def retention_dir(q, k, v, log_gamma, include_diag):
    b, L, H, dk = q.shape
    dv = v.shape[-1]
    n = L // RET_CHUNK

    def chunks(t):
        return t.reshape(b, n, RET_CHUNK, H, t.shape[-1]).transpose(1, 0, 3, 2, 4)

    pos = jnp.arange(RET_CHUNK, dtype=jnp.float32)
    diff = pos[:, None] - pos[None, :]
    mask = (diff >= 0) if include_diag else (diff > 0)
    intra_decay = jnp.where(mask, jnp.exp(log_gamma[:, None, None] * jnp.maximum(diff, 0.0)), 0.0)
    xi = jnp.exp(log_gamma[:, None] * (pos + 1.0))[..., None]
    zeta = jnp.exp(log_gamma[:, None] * (RET_CHUNK - 1.0 - pos))[..., None]
    chunk_decay = jnp.exp(log_gamma * RET_CHUNK)[:, None, None]

    def step(state, inp):
        qi, ki, vi = inp
        s = jnp.einsum('bhqd,bhkd->bhqk', qi, ki) * intra_decay
        out = jnp.einsum('bhqk,bhkv->bhqv', s, vi) + jnp.einsum('bhqd,bhdv->bhqv', qi * xi, state)
        state = state * chunk_decay + jnp.einsum('bhkd,bhkv->bhdv', ki * zeta, vi)
        return state, out

    init = jnp.zeros((b, H, dk, dv), jnp.float32)
    _, out = lax.scan(step, init, (chunks(q), chunks(k), chunks(v)))
    return out.transpose(1, 0, 3, 2, 4).reshape(b, L, H, dv)


def mla_attention(q, k, v):
    b, S, H, dq = q.shape
    nb = S // Q_BLOCK
    qb = q.reshape(b, nb, Q_BLOCK, H, dq).transpose(1, 0, 2, 3, 4)
    scale = dq ** -0.5

    def block(qi):
        s = jnp.einsum('bqhd,bkhd->bhqk', qi, k).astype(jnp.float32) * scale
        p = jax.nn.softmax(s, axis=-1)
        return jnp.einsum('bhqk,bkhv->bqhv', p.astype(v.dtype), v)

    o = lax.map(block, qb)
    return o.transpose(1, 0, 2, 3, 4).reshape(b, S, H * MLA_V)


def mixer_layer(x, norm_g, w_in, ret_gn_g, q_norm_g, kv_norm_g, w_uq, w_ukv, w_br_ret, w_br_mla, w_out):
    b, S, _ = x.shape
    h = rms_norm(x, norm_g)
    z = h @ w_in
    idx = [int(i) for i in np.cumsum(IN_SPLITS)[:-1]]
    rq, rk, rv, rg, cq, ckv, kpe, mg, g_ret, g_mla = jnp.split(z, idx, axis=-1)

    rq = rope(rq.reshape(b, S, RET_HEADS, RET_DK)).astype(jnp.float32)
    rk = rope(rk.reshape(b, S, RET_HEADS, RET_DK)).astype(jnp.float32) * (RET_DK ** -0.5)
    rv = rv.reshape(b, S, RET_HEADS, RET_DV).astype(jnp.float32)
    lg_f = jnp.log1p(-jnp.exp2(jnp.array(RET_LOG2_FWD, jnp.float32)))
    lg_b = jnp.log1p(-jnp.exp2(jnp.array(RET_LOG2_BWD, jnp.float32)))
    o_f = retention_dir(rq, rk, rv, lg_f, True)
    o_b = retention_dir(rq[:, ::-1], rk[:, ::-1], rv[:, ::-1], lg_b, False)[:, ::-1]
    o = o_f + o_b
    mu = jnp.mean(o, axis=-1, keepdims=True)
    var = jnp.mean(jnp.square(o - mu), axis=-1, keepdims=True)
    o = ((o - mu) * lax.rsqrt(var + NORM_EPS)).reshape(b, S, RET_WIDTH) * ret_gn_g.astype(jnp.float32)
    ret_out = (o.astype(x.dtype) * jax.nn.silu(rg)) @ w_br_ret

    cq = rms_norm(cq, q_norm_g)
    q = (cq @ w_uq).reshape(b, S, MLA_HEADS, MLA_QK)
    q = jnp.concatenate([q[..., :MLA_NOPE], rope(q[..., MLA_NOPE:])], axis=-1)
    ckv = rms_norm(ckv, kv_norm_g)
    kv = (ckv @ w_ukv).reshape(b, S, MLA_HEADS, MLA_NOPE + MLA_V)
    k_nope, v = kv[..., :MLA_NOPE], kv[..., MLA_NOPE:]
    k_pe = rope(kpe.reshape(b, S, 1, MLA_ROPE))
    k = jnp.concatenate([k_nope, jnp.broadcast_to(k_pe, (b, S, MLA_HEADS, MLA_ROPE))], axis=-1)
    a = mla_attention(q, k, v)
    mla_out = (a * jax.nn.silu(mg)) @ w_br_mla

    merged = jax.nn.sigmoid(g_ret) * ret_out + jax.nn.sigmoid(g_mla) * mla_out
    return x + merged @ w_out


def trunk(x, norm_g, w_in, ret_gn_g, q_norm_g, kv_norm_g, w_uq, w_ukv, w_br_ret, w_br_mla, w_out, final_norm_g):
    for i in range(DEPTH):
        x = mixer_layer(x, norm_g[i], w_in[i], ret_gn_g[i], q_norm_g[i], kv_norm_g[i], w_uq[i], w_ukv[i],
                        w_br_ret[i], w_br_mla[i], w_out[i])
    return rms_norm(x, final_norm_g)


def setup_inputs(seed: int = 0) -> dict:
    key = jax.random.key(seed)
    ks = jax.random.split(key, 16)
    f = jnp.float32

    def nrm(k, shape, fan_in):
        return jax.random.normal(k, shape, f) * (fan_in ** -0.5)

    def gain(k, shape):
        return 1.0 + 0.02 * jax.random.normal(k, shape, f)

    return {
        "x_prompt": jax.random.normal(ks[0], (BATCH, SEQ, D_MODEL), f),
        "x_sample": jax.random.normal(ks[1], (DEC_BATCH, DEC_SEQ, D_MODEL), f),
        "norm_g": gain(ks[2], (DEPTH, D_MODEL)),
        "w_in": nrm(ks[3], (DEPTH, D_MODEL, IN_WIDTH), D_MODEL),
        "ret_gn_g": gain(ks[4], (DEPTH, RET_WIDTH)),
        "q_norm_g": gain(ks[5], (DEPTH, Q_LORA)),
        "kv_norm_g": gain(ks[6], (DEPTH, KV_LORA)),
        "w_uq": nrm(ks[7], (DEPTH, Q_LORA, MLA_HEADS * MLA_QK), Q_LORA),
        "w_ukv": nrm(ks[8], (DEPTH, KV_LORA, MLA_HEADS * (MLA_NOPE + MLA_V)), KV_LORA),
        "w_br_ret": nrm(ks[9], (DEPTH, RET_WIDTH, D_MODEL), RET_WIDTH),
        "w_br_mla": nrm(ks[10], (DEPTH, MLA_WIDTH, D_MODEL), MLA_WIDTH),
        "w_out": nrm(ks[11], (DEPTH, D_MODEL, D_MODEL), D_MODEL),
        "final_norm_g": gain(ks[12], (D_MODEL,)),
    }


def reference(x_prompt, x_sample, norm_g, w_in, ret_gn_g, q_norm_g, kv_norm_g, w_uq, w_ukv, w_br_ret, w_br_mla, w_out, final_norm_g):
    y_prompt = trunk(x_prompt, norm_g, w_in, ret_gn_g, q_norm_g, kv_norm_g, w_uq, w_ukv, w_br_ret, w_br_mla, w_out, final_norm_g)
    y_sample = trunk(x_sample, norm_g, w_in, ret_gn_g, q_norm_g, kv_norm_g, w_uq, w_ukv, w_br_ret, w_br_mla, w_out, final_norm_g)
    return (y_prompt, y_sample)
```

```python
import numpy as np
import ml_dtypes
from contextlib import ExitStack
import concourse.bass as bass
import concourse.mybir as mybir
from concourse.bass_utils import run_bass_kernel_spmd

F32 = mybir.dt.float32
BF16 = mybir.dt.bfloat16
AF = mybir.ActivationFunctionType
ALU = mybir.AluOpType
AX = mybir.AxisListType

D = 1024
KC = 8
INW = 5056
C_RQ, C_RK, C_RV, C_RG, C_CQ, C_CKV, C_KPE, C_MG, C_GR, C_GM = 0, 512, 1024, 1536, 2048, 2304, 2432, 2496, 3008, 4032
EPS = 1e-6
RET_LOG2_FWD = (-5.0, -6.0, -7.0, -8.0)
RET_LOG2_BWD = (-5.5, -6.5, -7.5, -8.5)
ARENA = 212736
N_CORES = 8
SEQS_FULL = (2048, 2048, 2048, 2048, 4096)


def host_consts(smax):
    bf = ml_dtypes.bfloat16
    c = {}
    c["c_ident"] = np.eye(128, dtype=np.float32).astype(bf)
    c["c_ones"] = np.ones((128, 128), np.float32).astype(bf)
    pos = np.arange(smax, dtype=np.float32)
    inv_r = (np.float32(10000.0) ** (-(np.arange(0, 128, 2, dtype=np.float32) / np.float32(128)))).astype(np.float32)
    ang_r = (pos[:, None] * inv_r[None, :]).astype(np.float32).astype(np.float64)
    c["c_cosr"] = np.cos(ang_r).astype(np.float32)
    c["c_sinr"] = np.sin(ang_r).astype(np.float32)
    inv_m = (np.float32(10000.0) ** (-(np.arange(0, 64, 2, dtype=np.float32) / np.float32(64)))).astype(np.float32)
    ang_m = (pos[:, None] * inv_m[None, :]).astype(np.float32).astype(np.float64)
    idx = (np.arange(128) % 64) % 32
    c["c_cosm"] = np.ascontiguousarray(np.cos(ang_m)[:, idx].T).astype(np.float32)
    c["c_sinm"] = np.ascontiguousarray(np.sin(ang_m)[:, idx].T).astype(np.float32)
    lgf = np.log1p(-np.exp2(np.array(RET_LOG2_FWD, np.float64)))
    lgb = np.log1p(-np.exp2(np.array(RET_LOG2_BWD, np.float64)))
    k = np.arange(128)[:, None, None].astype(np.float64)
    q = np.arange(128)[None, None, :].astype(np.float64)
    diff = q - k
    M = np.where(diff >= 0, np.exp(lgf[None, :, None] * np.maximum(diff, 0)), np.exp(lgb[None, :, None] * np.maximum(-diff, 0)))
    c["c_mask"] = M.reshape(128, 512).astype(np.float32)
    qq = np.arange(128, dtype=np.float64)
    xif = np.exp(lgf[:, None] * (qq[None, :] + 1.0)).reshape(512)
    xib = np.exp(lgb[:, None] * (128.0 - qq[None, :])).reshape(512)
    decf = np.repeat(np.exp(lgf * 128.0), 128)
    decb = np.repeat(np.exp(lgb * 128.0), 128)
    c["c_rows"] = np.stack([xif, xib, decf, decb]).astype(np.float32)
    zf = np.exp(lgf[None, :] * (127.0 - qq[:, None]))
    zb = np.exp(lgb[None, :] * qq[:, None])
    c["c_zeta"] = np.concatenate([zf, zb], axis=1).astype(np.float32)
    return c


class Buf:
    __slots__ = ("w", "rd")

    def __init__(self):
        self.w = None
        self.rd = {}


def bufs(n):
    return [Buf() for _ in range(n)]


class Eng:
    def __init__(self, key, sem):
        self.key = key
        self.sem = sem
        self.n = 0
        self.ops = []
        self.known = {}


class Prog:
    def __init__(self, nc, stack, n_dma_sems=24):
        self.nc = nc
        self.sems = []
        self.E = {}
        for key in ("pe", "act", "dve", "pool", "sp"):
            s = stack.enter_context(nc.semaphore("s_" + key))
            self.sems.append(s)
            self.E[key] = Eng(key, len(self.sems) - 1)
        self.dq = {}
        for key in ("sp", "pool"):
            ids = []
            for i in range(n_dma_sems):
                s = stack.enter_context(nc.semaphore("d_%s%d" % (key, i)))
                self.sems.append(s)
                ids.append(len(self.sems) - 1)
            self.dq[key] = {"ids": ids, "vals": [0] * n_dma_sems, "next": 0}

    def _need(self, reads, writes):
        need = {}
        for b in reads:
            if b.w is not None and need.get(b.w[0], 0) < b.w[1]:
                need[b.w[0]] = b.w[1]
        for b in writes:
            if b.w is not None and need.get(b.w[0], 0) < b.w[1]:
                need[b.w[0]] = b.w[1]
            for s, v in b.rd.items():
                if need.get(s, 0) < v:
                    need[s] = v
        return need

    def _record(self, tok, reads, writes):
        for b in reads:
            if b.rd.get(tok[0], 0) < tok[1]:
                b.rd[tok[0]] = tok[1]
        for b in writes:
            b.w = tok
            b.rd = {}

    def op(self, ek, fn, reads=(), writes=()):
        eng = self.E[ek]
        need = self._need(reads, writes)
        if ek == "pe":
            need.pop(eng.sem, None)
        waits = []
        for s, v in need.items():
            if eng.known.get(s, 0) < v:
                eng.known[s] = v
                waits.append((s, v))
        eng.n += 1
        tok = (eng.sem, eng.n)
        eng.ops.append((waits, fn, (eng.sem, 1)))
        self._record(tok, reads, writes)
        return tok

    def dma(self, out, in_, reads=(), writes=(), q="sp", slow=False):
        eng = self.E[q]
        dq = self.dq[q]
        j = dq["next"] % len(dq["ids"])
        dq["next"] += 1
        sid = dq["ids"][j]
        need = self._need(reads, writes)
        prev = dq["vals"][j]
        if prev and need.get(sid, 0) < prev:
            need[sid] = prev
        waits = []
        for s, v in need.items():
            if eng.known.get(s, 0) < v:
                eng.known[s] = v
                waits.append((s, v))
        dq["vals"][j] = prev + 16
        tok = (sid, prev + 16)
        if slow:
            eng.ops.append((waits, (lambda e, o=out, i=in_: e.dma_start(out=o, in_=i, allow_slow_non_contiguous=True)), (sid, 16)))
        else:
            eng.ops.append((waits, (lambda e, o=out, i=in_: e.dma_start(out=o, in_=i)), (sid, 16)))
        self._record(tok, reads, writes)
        return tok

    def barrier(self):
        toks = {}
        for e in self.E.values():
            if e.n:
                toks[e.sem] = e.n
        for dq in self.dq.values():
            for sid, v in zip(dq["ids"], dq["vals"]):
                if v:
                    toks[sid] = v
        for e in self.E.values():
            waits = []
            for s, v in toks.items():
                if e.known.get(s, 0) < v:
                    e.known[s] = v
                    waits.append((s, v))
            if waits:
                e.ops.append((waits, None, None))

    def emit(self):
        sems = self.sems
        with self.nc.Block() as block:
            for key, deco in (("pe", block.tensor), ("act", block.scalar), ("dve", block.vector),
                              ("pool", block.gpsimd), ("sp", block.sync)):
                ops = self.E[key].ops

                def body(e, ops=ops):
                    for waits, fn, inc in ops:
                        for s, v in waits:
                            e.wait_ge(sems[s], v)
                        if fn is not None:
                            ins = fn(e)
                            ins.then_inc(sems[inc[0]], inc[1])
                deco(body)


class Arena:
    def __init__(self, t, nbytes):
        self.t = t
        self.n = nbytes
        self.off = 0
        self.peak = 0

    def mark(self):
        return self.off

    def release(self, m):
        self.off = m

    def alloc(self, dtype, *free):
        n = 1
        for f in free:
            n *= f
        nb = n * (4 if dtype == F32 else 2)
        off = (self.off + 63) // 64 * 64
        self.off = off + nb
        self.peak = max(self.peak, self.off)
        assert self.off <= self.n, "SBUF arena overflow: %d > %d" % (self.off, self.n)
        a = self.t[:, off // 2:(off + nb) // 2]
        if dtype == F32:
            a = a.bitcast(F32)
        if len(free) == 2:
            a = a.rearrange("p (a b) -> p a b", a=free[0])
        elif len(free) == 3:
            a = a.rearrange("p (a b c) -> p a b c", a=free[0], b=free[1])
        return a


def v3(ap, a):
    return ap.rearrange("p (a b) -> p a b", a=a)


def v4(ap, a, b):
    return ap.rearrange("p (a b c) -> p a b c", a=a, b=b)


def build_program(seqs, n_layers=2, debug=False, stop_after=None):
    T = sum(seqs)
    NTt = T // 128
    SMAX = max(seqs)
    assert all(s % 512 == 0 for s in seqs)
    nc = bass.Bass("TRN2", target_bir_lowering=False)
    dt = nc.dram_tensor

    def din(name, shape, dtype=F32):
        return dt(name, list(shape), dtype, kind="ExternalInput").ap()

    def scr(name, shape, dtype):
        return dt(name, list(shape), dtype, kind=("ExternalOutput" if debug else "Internal")).ap()

    X = din("x", [T, D])
    Y = dt("y", [T, D], F32, kind="ExternalOutput").ap()
    W_norm = din("norm_g", [2, D])
    W_in = din("w_in", [2, D, INW])
    W_gn = din("ret_gn_g", [2, 512])
    W_qn = din("q_norm_g", [2, 256])
    W_kvn = din("kv_norm_g", [2, 128])
    W_uq = din("w_uq", [2, 256, 768])
    W_ukv = din("w_ukv", [2, 128, 1024])
    W_brr = din("w_br_ret", [2, 512, D])
    W_brm = din("w_br_mla", [2, 512, D])
    W_out = din("w_out", [2, D, D])
    W_fin = din("final_norm_g", [D])
    Cd = {
        "ident": din("c_ident", [128, 128], BF16), "ones": din("c_ones", [128, 128], BF16),
        "cosr": din("c_cosr", [SMAX, 64]), "sinr": din("c_sinr", [SMAX, 64]),
        "cosm": din("c_cosm", [128, SMAX]), "sinm": din("c_sinm", [128, SMAX]),
        "mask": din("c_mask", [128, 512]), "rows": din("c_rows", [4, 512]), "zeta": din("c_zeta", [128, 8]),
    }
    XMID = scr("xmid", [T, D], F32)
    QTN = scr("s_qtn", [4, 128, T], BF16)
    QTP = scr("s_qtp", [2, 128, T], BF16)
    KTN = scr("s_ktn", [4, 128, T], BF16)
    KTP = scr("s_ktp", [2, 128, T], BF16)
    VV = scr("s_v", [T, 512], BF16)
    QXF = scr("s_qxf", [4, 128, T], BF16)
    QXB = scr("s_qxb", [4, 128, T], BF16)
    OI = scr("s_oi", [T, 512], F32)
    UF = scr("s_uf", [NTt, 128, 512], F32)
    UB = scr("s_ub", [NTt, 128, 512], F32)
    SF = scr("s_sf", [NTt, 128, 512], BF16)
    SB = scr("s_sb", [NTt, 128, 512], BF16)
    RGT = scr("s_rgt", [4, 128, T], BF16)
    MGT = scr("s_mgt", [4, 128, T], BF16)
    GRT = scr("s_grt", [8, 128, T], BF16)
    GMT = scr("s_gmt", [8, 128, T], BF16)
    RETG = scr("s_retg", [4, 128, T], BF16)
    AG = scr("s_ag", [4, 128, T], BF16)

    with ExitStack() as stack:
        arena_t = stack.enter_context(nc.sbuf_tensor("arena", [128, ARENA // 2], BF16))
        ps_t = stack.enter_context(nc.psum_tensor("ps", [128, 8, 512], F32))
        P = Prog(nc, stack)
        A = Arena(arena_t, ARENA)

        def bank(i):
            return ps_t[:, i, :]

        def bankb(i):
            return ps_t[:, i, :].bitcast(BF16)

        ident = A.alloc(BF16, 128)
        ones = A.alloc(BF16, 128)
        zeta = A.alloc(F32, 8)
        cst = A.alloc(F32, 8)
        stats = A.alloc(F32, 64)
        stat_bufs = bufs(64)
        stat_ctr = [0]

        def stat(n=1):
            i = stat_ctr[0]
            if (i % 64) + n > 64:
                i = (i // 64 + 1) * 64
            stat_ctr[0] = i + n
            i %= 64
            return stats[:, i:i + n], stat_bufs[i:i + n]

        P.dma(ident, Cd["ident"][:, :])
        P.dma(ones, Cd["ones"][:, :])
        P.dma(zeta, Cd["zeta"][:, :])
        P.op("pool", lambda e: e.memset(cst[:, 0:1], D * EPS))
        P.op("pool", lambda e: e.memset(cst[:, 1:2], 256 * EPS))
        P.op("pool", lambda e: e.memset(cst[:, 2:3], 128 * EPS))
        P.op("pool", lambda e: e.memset(cst[:, 3:6], -0.5))
        P.op("pool", lambda e: e.memset(cst[:, 6:7], 128 * EPS))
        P.barrier()
        base_mark = A.mark()

        def rstd_ops(ss_ap, ss_b, out_ap, out_b, c0, n=1):
            tmp, tmp_b = stat(n)
            P.op("pool", lambda e: e.tensor_tensor(out=tmp, in0=ss_ap, in1=cst[:, c0:c0 + n], op=ALU.add),
                 reads=ss_b, writes=tmp_b)
            P.op("pool", lambda e: e.tensor_tensor(out=out_ap, in0=tmp, in1=cst[:, 3:3 + n], op=ALU.pow),
                 reads=tmp_b, writes=out_b)

        seq_off = []
        o = 0
        for s in seqs:
            seq_off.append(o)
            o += s

        for L in range(n_layers):
            Xin = X if L == 0 else XMID
            last = (L == n_layers - 1)
            A.release(base_mark)
            Wg = A.alloc(BF16, KC, INW)
            Wkpe = A.alloc(BF16, KC, 128)
            Wkpr = A.alloc(BF16, KC, 128)
            Wuq = A.alloc(BF16, 2, 768)
            Wqpe = A.alloc(BF16, 2, 2, 128)
            Wqpr = A.alloc(BF16, 2, 2, 128)
            Wukv = A.alloc(BF16, 1024)
            Wv = A.alloc(BF16, 512)
            mask = A.alloc(F32, 512)
            xif = A.alloc(F32, 512)
            xib = A.alloc(F32, 512)
            P.dma(mask, Cd["mask"][:, :])
            P.dma(xif, Cd["rows"][0].partition_broadcast(128))
            P.dma(xib, Cd["rows"][1].partition_broadcast(128))
            gcol = A.alloc(F32, 8)
            qgcol = A.alloc(F32, 2)
            kvgcol = A.alloc(F32, 1)
            wA_mark = A.mark()
            stg = [A.alloc(F32, 2528), A.alloc(F32, 2528)]
            stg_b = bufs(2)
            gcol_b, qg_b, kvg_b = Buf(), Buf(), Buf()
            P.dma(gcol, W_norm[L].rearrange("(c p) -> p c", p=128), writes=[gcol_b], slow=True)
            P.dma(qgcol, W_qn[L].rearrange("(c p) -> p c", p=128), writes=[qg_b], slow=True)
            P.dma(kvgcol, W_kvn[L].rearrange("(c p) -> p c", p=128), writes=[kvg_b], slow=True)
            P.op("dve", lambda e: e.tensor_scalar(out=gcol, in0=gcol, scalar1=32.0, scalar2=None, op0=ALU.mult),
                 reads=[gcol_b], writes=[gcol_b])
            P.op("dve", lambda e: e.tensor_scalar(out=qgcol, in0=qgcol, scalar1=16.0, scalar2=None, op0=ALU.mult),
                 reads=[qg_b], writes=[qg_b])
            P.op("dve", lambda e: e.tensor_scalar(out=kvgcol, in0=kvgcol, scalar1=float(np.sqrt(128.0)), scalar2=None, op0=ALU.mult),
                 reads=[kvg_b], writes=[kvg_b])
            wi = 0
            for kc in range(KC):
                for half in range(2):
                    sb = stg_b[wi % 2]
                    st_ap = stg[wi % 2]
                    c0 = half * 2528
                    P.dma(st_ap, W_in[L, kc * 128:(kc + 1) * 128, c0:c0 + 2528], writes=[sb])
                    if half == 0:
                        P.op("act", lambda e, kc=kc, st_ap=st_ap: e.activation(out=Wg[:, kc, 0:512], in_=st_ap[:, 0:512], func=AF.Copy, scale=gcol[:, kc:kc + 1]),
                             reads=[sb, gcol_b])
                        P.op("dve", lambda e, kc=kc, st_ap=st_ap: e.tensor_scalar(out=Wg[:, kc, 512:1024], in0=st_ap[:, 512:1024], scalar1=gcol[:, kc:kc + 1], scalar2=float(128.0 ** -0.5), op0=ALU.mult, op1=ALU.mult),
                             reads=[sb, gcol_b])
                        P.op("act", lambda e, kc=kc, st_ap=st_ap: e.activation(out=Wg[:, kc, 1024:2528], in_=st_ap[:, 1024:2528], func=AF.Copy, scale=gcol[:, kc:kc + 1]),
                             reads=[sb, gcol_b])
                    else:
                        P.op("dve", lambda e, kc=kc, st_ap=st_ap: e.tensor_scalar(out=Wg[:, kc, 2528:3792], in0=st_ap[:, 0:1264], scalar1=gcol[:, kc:kc + 1], scalar2=None, op0=ALU.mult),
                             reads=[sb, gcol_b])
                        P.op("act", lambda e, kc=kc, st_ap=st_ap: e.activation(out=Wg[:, kc, 3792:5056], in_=st_ap[:, 1264:2528], func=AF.Copy, scale=gcol[:, kc:kc + 1]),
                             reads=[sb, gcol_b])
                    wi += 1
            P.barrier()
            kp = Wg[:, :, C_KPE:C_KPE + 64]
            for hf in range(2):
                P.op("dve", lambda e, hf=hf: e.tensor_copy(out=Wkpe[:, :, hf * 64:hf * 64 + 64], in_=kp))
                P.op("dve", lambda e, hf=hf: e.tensor_scalar(out=Wkpr[:, :, hf * 64:hf * 64 + 32], in0=Wg[:, :, C_KPE + 32:C_KPE + 64], scalar1=-1.0, scalar2=None, op0=ALU.mult))
                P.op("dve", lambda e, hf=hf: e.tensor_copy(out=Wkpr[:, :, hf * 64 + 32:hf * 64 + 64], in_=Wg[:, :, C_KPE:C_KPE + 32]))
            for kc in range(2):
                sb = stg_b[wi % 2]
                st_ap = stg[wi % 2]
                wi += 1
                P.dma(st_ap[:, 0:768], W_uq[L, kc * 128:(kc + 1) * 128, :], writes=[sb])
                P.op("act", lambda e, kc=kc, st_ap=st_ap: e.activation(out=Wuq[:, kc, :], in_=st_ap[:, 0:768], func=AF.Copy, scale=qgcol[:, kc:kc + 1]),
                     reads=[sb, qg_b], writes=[sb])
            P.barrier()
            for kc in range(2):
                for h in range(4):
                    pr, hh = h // 2, h % 2
                    src = Wuq[:, kc, h * 192 + 128:h * 192 + 192]
                    P.op("dve", lambda e, kc=kc, pr=pr, hh=hh, src=src: e.tensor_copy(out=Wqpe[:, kc, pr, hh * 64:hh * 64 + 64], in_=src))
                    P.op("dve", lambda e, kc=kc, pr=pr, hh=hh, src=src: e.tensor_scalar(out=Wqpr[:, kc, pr, hh * 64:hh * 64 + 32], in0=src[:, 32:64], scalar1=-1.0, scalar2=None, op0=ALU.mult))
                    P.op("dve", lambda e, kc=kc, pr=pr, hh=hh, src=src: e.tensor_copy(out=Wqpr[:, kc, pr, hh * 64 + 32:hh * 64 + 64], in_=src[:, 0:32]))
            sb = stg_b[wi % 2]
            st_ap = stg[wi % 2]
            wi += 1
            P.dma(st_ap[:, 0:1024], W_ukv[L, :, :], writes=[sb])
            P.op("act", lambda e, st_ap=st_ap: e.activation(out=Wukv, in_=st_ap[:, 0:1024], func=AF.Copy, scale=kvgcol[:, 0:1]),
                 reads=[sb, kvg_b], writes=[sb])
            P.barrier()
            P.op("dve", lambda e: e.tensor_copy(out=v3(Wv, 4), in_=v4(Wukv, 4, 2)[:, :, 1, :]))
            P.barrier()
            A.release(wA_mark)
            if stop_after == "Aw":
                break

            xs = [A.alloc(F32, 1024) for _ in range(2)]
            xs_b = bufs(2)
            hb = [A.alloc(BF16, 1024) for _ in range(2)]
            hb_b = bufs(2)
            hT = [A.alloc(BF16, KC, 512) for _ in range(2)]
            hT_b = [bufs(4), bufs(4)]
            tabr = [(A.alloc(F32, 4, 64), A.alloc(F32, 4, 64)) for _ in range(2)]
            tabr_b = [bufs(2), bufs(2)]
            tabm = [(A.alloc(F32, 512), A.alloc(F32, 512)) for _ in range(2)]
            tabm_b = [bufs(2), bufs(2)]
            qkr = [A.alloc(BF16, 1024) for _ in range(2)]
            qkr_b = [bufs(2), bufs(2)]
            rt = [A.alloc(F32, 4, 256) for _ in range(2)]
            rt_b = [bufs(4), bufs(4)]
            vb = [A.alloc(BF16, 512) for _ in range(2)]
            vb_b = bufs(2)
            cn = [A.alloc(BF16, 384) for _ in range(2)]
            cn_b = [bufs(2), bufs(2)]
            QTt = [A.alloc(BF16, 512) for _ in range(2)]
            QTt_b = bufs(2)
            KTt = [A.alloc(BF16, 512) for _ in range(2)]
            KTt_b = bufs(2)
            Kz = [A.alloc(BF16, 2, 512) for _ in range(2)]
            Kz_b = [bufs(2), bufs(2)]
            Pm = [A.alloc(BF16, 512) for _ in range(2)]
            Pm_b = bufs(2)
            OIs = [A.alloc(F32, 512) for _ in range(2)]
            OIs_b = bufs(2)
            Us_1 = A.alloc(F32, 2, 512)
            Us = [Us_1, Us_1]
            Us_1b = bufs(2)
            Us_b = [Us_1b, Us_1b]
            qxs = (A.alloc(BF16, 4, 512), A.alloc(BF16, 4, 512))
            qxs_b = [bufs(4), bufs(4)]
            cqT = A.alloc(BF16, 2, 512)
            cqT_b = bufs(4)
            ckvT = A.alloc(BF16, 512)
            ckvT_b = bufs(4)
            Vs = A.alloc(BF16, 4, 512)
            Vs_b = bufs(4)
            QTPs = A.alloc(BF16, 2, 512)
            QTPs_b = bufs(2)
            KTPs = A.alloc(BF16, 2, 512)
            KTPs_b = bufs(1)
            P.op("pool", lambda e: e.memset(KTPs, 0.0), writes=KTPs_b)
            NSLOT = 5
            slots = [A.alloc(BF16, 512) for _ in range(NSLOT)]
            slots_b = bufs(NSLOT)
            sgt = [A.alloc(F32, 512) for _ in range(2)]
            sgt_b = bufs(2)
            mrt = [A.alloc(F32, 512) for _ in range(2)]
            mrt_b = bufs(2)
            Tb = bufs(2)
            Zb = bufs(4)
            Rb = bufs(2)
            ctr = {"T": 0, "Z": 0, "R": 0, "slot": 0, "sg": 0, "mr": 0, "tile": 0}

            def nxt(k, n):
                i = ctr[k] % n
                ctr[k] += 1
                return i

            def zbank():
                i = nxt("Z", 4)
                return bank(2 + i), Zb[i]

            def tbank():
                i = nxt("T", 2)
                return bankb(i), Tb[i]

            def rbank():
                i = nxt("R", 2)
                return bank(6 + i), Rb[i]

            def slot():
                i = nxt("slot", NSLOT)
                return slots[i], slots_b[i]

            glist = []
            for si, s in enumerate(seqs):
                for gi in range(s // 512):
                    glist.append((si, seq_off[si] + gi * 512, gi * 512))
            NG = len(glist)

            def s0_load(g, j):
                _, tok0, pos0 = glist[g]
                ti = g * 4 + j
                sl = ti % 2
                if j == 0:
                    tr, trb = tabr[g % 2], tabr_b[g % 2]
                    tm, tmb = tabm[g % 2], tabm_b[g % 2]
                    P.dma(tr[0], Cd["cosr"][pos0:pos0 + 512, :].rearrange("(j p) i -> p j i", p=128), writes=[trb[0]])
                    P.dma(tr[1], Cd["sinr"][pos0:pos0 + 512, :].rearrange("(j p) i -> p j i", p=128), writes=[trb[1]])
                    P.dma(tm[0], Cd["cosm"][:, pos0:pos0 + 512], writes=[tmb[0]])
                    P.dma(tm[1], Cd["sinm"][:, pos0:pos0 + 512], writes=[tmb[1]])
                P.dma(xs[sl], Xin[tok0 + j * 128:tok0 + (j + 1) * 128, :], writes=[xs_b[sl]])

            def s0_norm(g, j):
                ti = g * 4 + j
                sl = ti % 2
                ss, ssb = stat()
                P.op("act", lambda e: e.activation(out=hb[sl], in_=xs[sl], func=AF.Square, accum_out=ss),
                     reads=[xs_b[sl]], writes=ssb + [hb_b[sl]])
                rs, rsb = stat()
                rstd_ops(ss, ssb, rs, rsb, 0)
                P.op("act", lambda e: e.activation(out=hb[sl], in_=xs[sl], func=AF.Copy, scale=rs),
                     reads=[xs_b[sl]] + rsb, writes=[hb_b[sl]])

            def s0_pre(g, j):
                s0_load(g, j)
                s0_norm(g, j)

            def s0_pe(g, j):
                ti = g * 4 + j
                sl = ti % 2
                tb_ap, tbb = tbank()

                def f(e):
                    for kc in range(KC):
                        ins = e.transpose(tb_ap[:, kc * 128:(kc + 1) * 128], hb[sl][:, kc * 128:(kc + 1) * 128], ident)
                    return ins
                P.op("pe", f, reads=[hb_b[sl]], writes=[tbb])
                dst = hT[g % 2][:, :, j * 128:(j + 1) * 128]
                P.op("act", lambda e: e.activation(out=dst, in_=v3(tb_ap, KC), func=AF.Copy),
                     reads=[tbb], writes=[hT_b[g % 2][j]])

            def mm8_fm(g, c0, m, zb_ap, wsrc=None):
                hTg = hT[g % 2]

                def f(e):
                    for kc in range(KC):
                        lhs = (Wg[:, kc, c0:c0 + m] if wsrc is None else wsrc[:, kc, :])
                        ins = e.matmul(zb_ap[0:m, :], lhs, hTg[:, kc, :], start=(kc == 0), stop=(kc == KC - 1))
                    return ins
                return f

            def mm8_tm(g, j, c0, n, zb_ap):
                hTg = hT[g % 2]

                def f(e):
                    for kc in range(KC):
                        ins = e.matmul(zb_ap[:, 0:n], hTg[:, kc, j * 128:(j + 1) * 128], Wg[:, kc, c0:c0 + n], start=(kc == 0), stop=(kc == KC - 1))
                    return ins
                return f

            def fm_gate(g, kind, idx):
                _, tok0, _ = glist[g]
                c0 = {"rg": C_RG, "mg": C_MG, "gr": C_GR, "gm": C_GM}[kind] + idx * 128
                dst = {"rg": RGT, "mg": MGT, "gr": GRT, "gm": GMT}[kind][idx, :, tok0:tok0 + 512]
                zb_ap, zbb = zbank()
                P.op("pe", mm8_fm(g, c0, 128, zb_ap), reads=hT_b[g % 2], writes=[zbb])
                sl_ap, slb = slot()
                if kind in ("gr", "gm"):
                    P.op("act", lambda e: e.activation(out=sl_ap, in_=zb_ap, func=AF.Sigmoid), reads=[zbb], writes=[slb])
                else:
                    i = nxt("sg", 2)
                    P.op("act", lambda e: e.activation(out=sgt[i], in_=zb_ap, func=AF.Sigmoid), reads=[zbb], writes=[sgt_b[i]])
                    P.op("dve", lambda e: e.tensor_tensor(out=sl_ap, in0=zb_ap, in1=sgt[i], op=ALU.mult),
                         reads=[zbb, sgt_b[i]], writes=[slb])
                P.dma(dst, sl_ap, reads=[slb])

            def fm_kpe(g):
                _, tok0, _ = glist[g]
                za, zab = zbank()
                P.op("pe", mm8_fm(g, 0, 128, za, wsrc=Wkpe), reads=hT_b[g % 2], writes=[zab])
                zr, zrb = zbank()
                P.op("pe", mm8_fm(g, 0, 128, zr, wsrc=Wkpr), reads=hT_b[g % 2], writes=[zrb])
                tm, tmb = tabm[g % 2], tabm_b[g % 2]
                i0, i1 = nxt("mr", 2), nxt("mr", 2)
                P.op("dve", lambda e: e.tensor_tensor(out=mrt[i0], in0=za, in1=tm[0], op=ALU.mult), reads=[zab, tmb[0]], writes=[mrt_b[i0]])
                P.op("dve", lambda e: e.tensor_tensor(out=mrt[i1], in0=zr, in1=tm[1], op=ALU.mult), reads=[zrb, tmb[1]], writes=[mrt_b[i1]])
                P.op("pool", lambda e: e.tensor_tensor(out=KTPs[0:64, 0, :], in0=mrt[i0][0:64, :], in1=mrt[i1][0:64, :], op=ALU.add),
                     reads=[mrt_b[i0], mrt_b[i1]], writes=KTPs_b)
                P.op("pool", lambda e: e.tensor_tensor(out=KTPs[64:128, 1, :], in0=mrt[i0][64:128, :], in1=mrt[i1][64:128, :], op=ALU.add),
                     reads=[mrt_b[i0], mrt_b[i1]], writes=KTPs_b)
                P.dma(KTP[:, :, tok0:tok0 + 512].rearrange("v p t -> p v t"), KTPs, reads=KTPs_b)

            def tm_rope(g, j, which):
                sl = (g * 4 + j) % 2
                zb_ap, zbb = zbank()
                P.op("pe", mm8_tm(g, j, (C_RQ, C_RK)[which], 512, zb_ap), reads=[hT_b[g % 2][j]], writes=[zbb])
                zv = v4(zb_ap, 4, 2)
                tr, trb = tabr[g % 2], tabr_b[g % 2]
                cosb = tr[0][:, j, :].unsqueeze(1).to_broadcast([128, 4, 64])
                sinb = tr[1][:, j, :].unsqueeze(1).to_broadcast([128, 4, 64])
                t = [v3(rt[sl][:, i, :], 4) for i in range(4)]
                tb_ = rt_b[sl]
                outv = v4(qkr[sl][:, which * 512:(which + 1) * 512], 4, 2)
                ob = qkr_b[sl][which]
                P.op("dve", lambda e: e.tensor_tensor(out=t[0], in0=zv[:, :, 0, :], in1=cosb, op=ALU.mult), reads=[zbb, trb[0]], writes=[tb_[0]])
                P.op("dve", lambda e: e.tensor_tensor(out=t[1], in0=zv[:, :, 1, :], in1=sinb, op=ALU.mult), reads=[zbb, trb[1]], writes=[tb_[1]])
                P.op("dve", lambda e: e.tensor_tensor(out=t[2], in0=zv[:, :, 1, :], in1=cosb, op=ALU.mult), reads=[zbb, trb[0]], writes=[tb_[2]])
                P.op("dve", lambda e: e.tensor_tensor(out=t[3], in0=zv[:, :, 0, :], in1=sinb, op=ALU.mult), reads=[zbb, trb[1]], writes=[tb_[3]])
                P.op("pool", lambda e: e.tensor_tensor(out=outv[:, :, 0, :], in0=t[0], in1=t[1], op=ALU.subtract), reads=[tb_[0], tb_[1]], writes=[ob])
                P.op("pool", lambda e: e.tensor_tensor(out=outv[:, :, 1, :], in0=t[2], in1=t[3], op=ALU.add), reads=[tb_[2], tb_[3]], writes=[ob])
                if which == 1:
                    kv_ = v3(qkr[sl][:, 512:1024], 4)
                    for d_ in range(2):
                        zt = zeta[:, d_ * 4:d_ * 4 + 4].unsqueeze(2).to_broadcast([128, 4, 128])
                        P.op("pool", lambda e, d_=d_, zt=zt: e.tensor_tensor(out=v3(Kz[sl][:, d_, :], 4), in0=kv_, in1=zt, op=ALU.mult),
                             reads=[ob], writes=[Kz_b[sl][d_]])

            def tm_v(g, j):
                sl = (g * 4 + j) % 2
                zb_ap, zbb = zbank()
                P.op("pe", mm8_tm(g, j, C_RV, 512, zb_ap), reads=[hT_b[g % 2][j]], writes=[zbb])
                P.op("act", lambda e: e.activation(out=vb[sl], in_=zb_ap, func=AF.Copy), reads=[zbb], writes=[vb_b[sl]])

            def tm_c(g, j):
                sl = (g * 4 + j) % 2
                zb_ap, zbb = zbank()
                P.op("pe", mm8_tm(g, j, C_CQ, 384, zb_ap), reads=[hT_b[g % 2][j]], writes=[zbb])
                ss, ssb = stat(2)
                P.op("act", lambda e: e.activation(out=cn[sl][:, 0:256], in_=zb_ap[:, 0:256], func=AF.Square, accum_out=ss[:, 0:1]), reads=[zbb], writes=[ssb[0], cn_b[sl][0]])
                P.op("act", lambda e: e.activation(out=cn[sl][:, 256:384], in_=zb_ap[:, 256:384], func=AF.Square, accum_out=ss[:, 1:2]), reads=[zbb], writes=[ssb[1], cn_b[sl][1]])
                rs, rsb = stat(2)
                rstd_ops(ss, ssb, rs, rsb, 1, n=2)
                P.op("act", lambda e: e.activation(out=cn[sl][:, 0:256], in_=zb_ap[:, 0:256], func=AF.Copy, scale=rs[:, 0:1]), reads=[zbb] + rsb, writes=[cn_b[sl][0]])
                P.op("act", lambda e: e.activation(out=cn[sl][:, 256:384], in_=zb_ap[:, 256:384], func=AF.Copy, scale=rs[:, 1:2]), reads=[zbb] + rsb, writes=[cn_b[sl][1]])

            def post1(g, j, parts="abc"):
                sl = (g * 4 + j) % 2
                tb_ap, tbb = tbank()

                def f(e):
                    for i in range(8):
                        ins = e.transpose(tb_ap[:, i * 128:(i + 1) * 128], qkr[sl][:, i * 128:(i + 1) * 128], ident)
                    return ins
                P.op("pe", f, reads=qkr_b[sl], writes=[tbb])
                P.op("act", lambda e: e.activation(out=QTt[sl], in_=tb_ap[:, 0:512], func=AF.Copy), reads=[tbb], writes=[QTt_b[sl]])
                P.op("act", lambda e: e.activation(out=KTt[sl], in_=tb_ap[:, 512:1024], func=AF.Copy), reads=[tbb], writes=[KTt_b[sl]])
                if "b" in parts:
                    extra = []
                    P.op("dve", lambda e: e.tensor_tensor(out=qxs[0][:, :, j * 128:(j + 1) * 128], in0=v3(QTt[sl], 4), in1=v3(xif, 4), op=ALU.mult),
                         reads=[QTt_b[sl]] + extra, writes=[qxs_b[0][j]])
                    P.op("dve", lambda e: e.tensor_tensor(out=qxs[1][:, :, j * 128:(j + 1) * 128], in0=v3(QTt[sl], 4), in1=v3(xib, 4), op=ALU.mult),
                         reads=[QTt_b[sl]] + extra, writes=[qxs_b[1][j]])
                if "c" not in parts:
                    return
                tc_ap, tcb = tbank()

                def f2(e):
                    for i in range(3):
                        ins = e.transpose(tc_ap[:, i * 128:(i + 1) * 128], cn[sl][:, i * 128:(i + 1) * 128], ident)
                    return ins
                P.op("pe", f2, reads=cn_b[sl], writes=[tcb])
                P.op("act", lambda e: e.activation(out=cqT[:, :, j * 128:(j + 1) * 128], in_=v3(tc_ap[:, 0:256], 2), func=AF.Copy), reads=[tcb], writes=[cqT_b[j]])
                P.op("act", lambda e: e.activation(out=ckvT[:, j * 128:(j + 1) * 128], in_=tc_ap[:, 256:384], func=AF.Copy), reads=[tcb], writes=[ckvT_b[j]])

            def post2(g, j):
                sl = (g * 4 + j) % 2
                _, tok0, _ = glist[g]
                tglob = (tok0 // 128) + j
                r_ap, rbb = rbank()

                def f(e):
                    for h in range(4):
                        ins = e.matmul(r_ap[:, h * 128:(h + 1) * 128], KTt[sl][:, h * 128:(h + 1) * 128], QTt[sl][:, h * 128:(h + 1) * 128], start=True, stop=True)
                    return ins
                P.op("pe", f, reads=[KTt_b[sl], QTt_b[sl]], writes=[rbb])
                P.op("dve", lambda e: e.tensor_tensor(out=Pm[sl], in0=r_ap, in1=mask, op=ALU.mult), reads=[rbb], writes=[Pm_b[sl]])
                for d_ in range(2):
                    u_ap, ubb = rbank()

                    def fu(e, d_=d_, u_ap=u_ap):
                        for h in range(4):
                            ins = e.matmul(u_ap[:, h * 128:(h + 1) * 128], Kz[sl][:, d_, h * 128:(h + 1) * 128], vb[sl][:, h * 128:(h + 1) * 128], start=True, stop=True)
                        return ins
                    P.op("pe", fu, reads=[Kz_b[sl][d_], vb_b[sl]], writes=[ubb])
                    P.op("dve", lambda e, d_=d_, u_ap=u_ap: e.tensor_copy(out=Us[sl][:, d_, :], in_=u_ap), reads=[ubb], writes=[Us_b[sl][d_]])
                    P.dma((UF, UB)[d_][tglob], Us[sl][:, d_, :], reads=[Us_b[sl][d_]])

            def post3(g, j):
                sl = (g * 4 + j) % 2
                _, tok0, _ = glist[g]
                o_ap, obb = rbank()

                def f(e):
                    for h in range(4):
                        ins = e.matmul(o_ap[:, h * 128:(h + 1) * 128], Pm[sl][:, h * 128:(h + 1) * 128], vb[sl][:, h * 128:(h + 1) * 128], start=True, stop=True)
                    return ins
                P.op("pe", f, reads=[Pm_b[sl], vb_b[sl]], writes=[obb])
                P.op("act", lambda e: e.activation(out=OIs[sl], in_=o_ap, func=AF.Copy), reads=[obb], writes=[OIs_b[sl]])
                P.dma(OI[tok0 + j * 128:tok0 + (j + 1) * 128, :], OIs[sl], reads=[OIs_b[sl]])
                zb_ap, zbb = zbank()
                P.op("pe", lambda e: e.matmul(zb_ap, ckvT[:, j * 128:(j + 1) * 128], Wv, start=True, stop=True),
                     reads=[ckvT_b[j]], writes=[zbb])
                P.op("act", lambda e: e.activation(out=Vs[:, j, :], in_=zb_ap, func=AF.Copy), reads=[zbb], writes=[Vs_b[j]])

            def s3_qn(g, h):
                _, tok0, _ = glist[g]
                zb_ap, zbb = zbank()

                def f(e):
                    for kc in range(2):
                        ins = e.matmul(zb_ap, Wuq[:, kc, h * 192:h * 192 + 128], cqT[:, kc, :], start=(kc == 0), stop=(kc == 1))
                    return ins
                P.op("pe", f, reads=cqT_b, writes=[zbb])
                sl_ap, slb = slot()
                P.op("act", lambda e: e.activation(out=sl_ap, in_=zb_ap, func=AF.Copy), reads=[zbb], writes=[slb])
                P.dma(QTN[h, :, tok0:tok0 + 512], sl_ap, reads=[slb])

            def s3_qp(g, pr):
                _, tok0, _ = glist[g]
                za, zab = zbank()
                zr, zrb = zbank()

                def fa(e):
                    for kc in range(2):
                        ins = e.matmul(za, Wqpe[:, kc, pr, :], cqT[:, kc, :], start=(kc == 0), stop=(kc == 1))
                    return ins

                def fr(e):
                    for kc in range(2):
                        ins = e.matmul(zr, Wqpr[:, kc, pr, :], cqT[:, kc, :], start=(kc == 0), stop=(kc == 1))
                    return ins
                P.op("pe", fa, reads=cqT_b, writes=[zab])
                P.op("pe", fr, reads=cqT_b, writes=[zrb])
                tm, tmb = tabm[g % 2], tabm_b[g % 2]
                i0, i1 = nxt("mr", 2), nxt("mr", 2)
                P.op("dve", lambda e: e.tensor_tensor(out=mrt[i0], in0=za, in1=tm[0], op=ALU.mult), reads=[zab, tmb[0]], writes=[mrt_b[i0]])
                P.op("dve", lambda e: e.tensor_tensor(out=mrt[i1], in0=zr, in1=tm[1], op=ALU.mult), reads=[zrb, tmb[1]], writes=[mrt_b[i1]])
                P.op("pool", lambda e: e.tensor_tensor(out=QTPs[:, pr, :], in0=mrt[i0], in1=mrt[i1], op=ALU.add),
                     reads=[mrt_b[i0], mrt_b[i1]], writes=[QTPs_b[pr]])
                P.dma(QTP[pr, :, tok0:tok0 + 512], QTPs[:, pr, :], reads=[QTPs_b[pr]])

            def s3_kn(g, h):
                _, tok0, _ = glist[g]
                zb_ap, zbb = zbank()
                P.op("pe", lambda e: e.matmul(zb_ap, Wukv[:, h * 256:h * 256 + 128], ckvT, start=True, stop=True), reads=ckvT_b, writes=[zbb])
                sl_ap, slb = slot()
                P.op("act", lambda e: e.activation(out=sl_ap, in_=zb_ap, func=AF.Copy), reads=[zbb], writes=[slb])
                P.dma(KTN[h, :, tok0:tok0 + 512], sl_ap, reads=[slb])

            def s3_stores(g):
                _, tok0, _ = glist[g]
                P.dma(QXF[:, :, tok0:tok0 + 512].rearrange("h p t -> p h t"), qxs[0], reads=qxs_b[0])
                P.dma(QXB[:, :, tok0:tok0 + 512].rearrange("h p t -> p h t"), qxs[1], reads=qxs_b[1])
                P.dma(VV[tok0:tok0 + 512, :].rearrange("(j p) c -> p j c", p=128), Vs, reads=Vs_b)

            for j in range(4):
                s0_pre(0, j)
                s0_pe(0, j)
            for g in range(NG):
                if stop_after == "A0":
                    break
                items = [lambda g=g: fm_kpe(g)]
                for kind, n in (("mg", 4), ("rg", 4), ("gr", 8), ("gm", 8)):
                    for i in range(n):
                        items.append(lambda g=g, kind=kind, i=i: fm_gate(g, kind, i))
                def run_item(n_):
                    if g + 1 < NG and n_ == 0:
                        s0_load(g + 1, 0)
                        s0_load(g + 1, 1)
                    items[n_]()
                    if g + 1 < NG:
                        if n_ == 9:
                            s0_norm(g + 1, 0)
                            s0_load(g + 1, 2)
                        elif n_ == 13:
                            s0_norm(g + 1, 1)
                            s0_load(g + 1, 3)
                        elif n_ == 17:
                            s0_pe(g + 1, 0)
                        elif n_ == 19:
                            s0_norm(g + 1, 2)
                        elif n_ == 21:
                            s0_pe(g + 1, 1)
                        elif n_ == 23:
                            s0_norm(g + 1, 3)
                for n_ in range(21):
                    run_item(n_)
                tm_rope(g, 0, 0)
                tm_rope(g, 0, 1)
                run_item(21)
                run_item(22)
                tm_v(g, 0)
                tm_c(g, 0)
                run_item(23)
                run_item(24)
                for j in range(1, 4):
                    if g + 1 < NG and j in (1, 2):
                        s0_pe(g + 1, j + 1)
                    tm_rope(g, j, 0)
                    tm_rope(g, j, 1)
                    post1(g, j - 1)
                    if j >= 2:
                        post3(g, j - 2)
                    tm_v(g, j)
                    tm_c(g, j)
                    post2(g, j - 1)
                post1(g, 3)
                post3(g, 2)
                post2(g, 3)
                for h in range(4):
                    s3_qn(g, h)
                post3(g, 3)
                for pr in range(2):
                    s3_qp(g, pr)
                for h in range(4):
                    s3_kn(g, h)
                s3_stores(g)
            P.barrier()
            if stop_after is not None and stop_after.startswith("A"):
                break

            A.release(base_mark)
            gng = A.alloc(F32, 4)
            gng_b = Buf()
            P.dma(gng, W_gn[L].rearrange("(h p) -> p h", p=128), writes=[gng_b], slow=True)
            P.op("dve", lambda e: e.tensor_scalar(out=gng, in0=gng, scalar1=float(np.sqrt(128.0)), scalar2=None, op0=ALU.mult),
                 reads=[gng_b], writes=[gng_b])
            decf = A.alloc(F32, 512)
            decb = A.alloc(F32, 512)
            P.dma(decf, Cd["rows"][2].partition_broadcast(128))
            P.dma(decb, Cd["rows"][3].partition_broadcast(128))
            P.barrier()
            Sst = [[A.alloc(F32, 512) for _ in range(2)] for _ in range(2)]
            Sst_b = [bufs(2), bufs(2)]
            NU = 4
            Ust = [[A.alloc(F32, 512) for _ in range(NU)] for _ in range(2)]
            Ust_b = [bufs(NU), bufs(NU)]
            Sn = [[A.alloc(BF16, 512) for _ in range(NU)] for _ in range(2)]
            Sn_b = [bufs(NU), bufs(NU)]
            SFd_b = bufs(NTt)
            SBd_b = bufs(NTt)

            def scan_load(d_, t, i):
                k = i % NU
                P.dma(Ust[d_][k], (UF, UB)[d_][t], writes=[Ust_b[d_][k]])

            def scan_step(d_, t, i, update):
                eng = ("dve", "pool")[d_]
                dec = (decf, decb)[d_]
                k = i % NU
                So, So_b = Sst[d_][i % 2], Sst_b[d_][i % 2]
                Sn_, Sn_b_ = Sst[d_][(i + 1) % 2], Sst_b[d_][(i + 1) % 2]
                U_ap, U_b = Ust[d_][k], Ust_b[d_][k]
                N_ap, N_b = Sn[d_][k], Sn_b[d_][k]
                P.op(eng, lambda e: e.tensor_copy(out=N_ap, in_=So), reads=[So_b], writes=[N_b])
                P.dma((SF, SB)[d_][t], N_ap, reads=[N_b], writes=[(SFd_b, SBd_b)[d_][t]])
                if update:
                    P.op(eng, lambda e: e.tensor_tensor(out=Sn_, in0=So, in1=dec, op=ALU.mult), reads=[So_b], writes=[Sn_b_])
                    P.op(eng, lambda e: e.tensor_tensor(out=Sn_, in0=Sn_, in1=U_ap, op=ALU.add), reads=[Sn_b_, U_b], writes=[Sn_b_])

            def scan_init(d_):
                eng = ("dve", "pool")[d_]
                S_ap, S_b = Sst[d_][0], Sst_b[d_][0]
                P.op(eng, lambda e: e.memset(S_ap, 0.0), writes=[S_b])

            def scan_list(si):
                nt_ = seqs[si] // 128
                t0_ = seq_off[si] // 128
                PF = NU - 1

                def tf(i):
                    return t0_ + i

                def tb(i):
                    return t0_ + nt_ - 1 - i

                def first():
                    scan_init(0)
                    scan_init(1)
                    for i in range(min(PF, nt_ - 1)):
                        scan_load(0, tf(i), i)
                        scan_load(1, tb(i), i)
                q = [first]
                for i in range(nt_):
                    def step(i=i):
                        if i + PF < nt_ - 1:
                            scan_load(0, tf(i + PF), i + PF)
                            scan_load(1, tb(i + PF), i + PF)
                        scan_step(0, tf(i), i, i < nt_ - 1)
                        scan_step(1, tb(i), i, i < nt_ - 1)
                    q.append(step)
                return q
            scan_q = []

            KTs = A.alloc(BF16, 4, SMAX)
            KPs = A.alloc(BF16, 2, SMAX)
            Vsq = A.alloc(BF16, SMAX // 128, 512)
            KT_b = bufs(4)
            KP_b = Buf()
            V_b = Buf()
            QN = [A.alloc(BF16, 4, 512) for _ in range(2)]
            QN_b = bufs(2)
            QP = [A.alloc(BF16, 2, 512) for _ in range(2)]
            QP_b = bufs(2)
            MG = [A.alloc(BF16, 4, 512) for _ in range(2)]
            MG_b = bufs(2)
            NPT = 6
            PT = [A.alloc(BF16, 512) for _ in range(NPT)]
            PT_b = bufs(NPT)
            cpend = [None]
            rec = A.alloc(F32, 512)
            rec_b = Buf()
            an = A.alloc(F32, 512)
            an_b = Buf()
            ags = [A.alloc(BF16, 512) for _ in range(2)]
            ags_b = bufs(2)
            _qxf, _qxb, _sfg, _sbg = A.alloc(BF16, 4, 512), A.alloc(BF16, 4, 512), A.alloc(BF16, 4, 512), A.alloc(BF16, 4, 512)
            QXf, QXb_, SFg, SBg = [_qxf, _qxf], [_qxb, _qxb], [_sfg, _sfg], [_sbg, _sbg]
            _qxbuf, _sgbuf = bufs(2), bufs(2)
            QX_b, SG_b = [_qxbuf, _qxbuf], [_sgbuf, _sgbuf]
            _oig, _rgg = A.alloc(F32, 4, 512), A.alloc(BF16, 4, 512)
            OIg, RGg = [_oig, _oig], [_rgg, _rgg]
            _oib, _rgb = Buf(), Buf()
            OIg_b, RGg_b = [_oib, _oib], [_rgb, _rgb]
            of = [A.alloc(F32, 512) for _ in range(2)]
            of_b = bufs(2)
            dd = [A.alloc(F32, 512) for _ in range(2)]
            dd_b = bufs(2)
            sq = A.alloc(F32, 512)
            sq_b = Buf()
            on = [A.alloc(BF16, 512) for _ in range(4)]
            on_b = bufs(4)
            tce = A.alloc(BF16, 512)
            tce_b = Buf()
            rgs = A.alloc(BF16, 4, 512)
            rgs_b = bufs(4)
            Sbk = bufs(3)
            Abk = bufs(2)
            Zbk = bufs(2)
            RObk = Buf()
            TCbk = RObk
            cc = {"S": 0, "AZ": 0, "PT": 0, "ag": 0, "t": 0}
            scale = float(192.0 ** -0.5)

            def c_ret1(gsl, j):
                ro = bank(7)
                t = j % 2

                def f(e):
                    for h in range(4):
                        e.matmul(ro[:, h * 128:(h + 1) * 128], QXf[gsl][:, h, j * 128:(j + 1) * 128], SFg[gsl][:, j, h * 128:(h + 1) * 128], start=True, stop=False)
                        ins = e.matmul(ro[:, h * 128:(h + 1) * 128], QXb_[gsl][:, h, j * 128:(j + 1) * 128], SBg[gsl][:, j, h * 128:(h + 1) * 128], start=False, stop=True)
                    return ins
                P.op("pe", f, reads=QX_b[gsl] + SG_b[gsl], writes=[RObk])
                P.op("dve", lambda e: e.tensor_tensor(out=of[t], in0=ro, in1=OIg[gsl][:, j, :], op=ALU.add), reads=[RObk, OIg_b[gsl]], writes=[of_b[t]])
                s1, s1b = stat(4)
                P.op("dve", lambda e: e.tensor_reduce(out=s1, in_=v3(of[t], 4), axis=AX.X, op=ALU.add), reads=[of_b[t]], writes=s1b)
                mu, mub = stat(4)
                P.op("dve", lambda e: e.tensor_scalar(out=mu, in0=s1, scalar1=1.0 / 128.0, scalar2=None, op0=ALU.mult), reads=s1b, writes=mub)
                P.op("dve", lambda e: e.tensor_tensor(out=v3(dd[t], 4), in0=v3(of[t], 4), in1=mu.unsqueeze(2).to_broadcast([128, 4, 128]), op=ALU.subtract),
                     reads=[of_b[t]] + mub, writes=[dd_b[t]])
                P.op("dve", lambda e: e.tensor_tensor(out=sq, in0=dd[t], in1=dd[t], op=ALU.mult), reads=[dd_b[t]], writes=[sq_b])
                s2, s2b = stat(4)
                P.op("dve", lambda e: e.tensor_reduce(out=s2, in_=v3(sq, 4), axis=AX.X, op=ALU.add), reads=[sq_b], writes=s2b)
                rs4, rs4b = stat(4)
                tmp4, tmp4b = stat(4)
                P.op("pool", lambda e: e.tensor_tensor(out=tmp4, in0=s2, in1=cst[:, 6:7].to_broadcast([128, 4]), op=ALU.add), reads=s2b, writes=tmp4b)
                P.op("pool", lambda e: e.tensor_tensor(out=rs4, in0=tmp4, in1=cst[:, 3:4].to_broadcast([128, 4]), op=ALU.pow), reads=tmp4b, writes=rs4b)
                P.op("pool", lambda e: e.tensor_tensor(out=v3(on[j], 4), in0=v3(dd[t], 4), in1=rs4.unsqueeze(2).to_broadcast([128, 4, 128]), op=ALU.mult),
                     reads=[dd_b[t]] + rs4b, writes=[on_b[j]])

            def c_ret2(gsl, j):
                tcb = bankb(7)

                def f2(e):
                    for h in range(4):
                        ins = e.transpose(tcb[:, h * 128:(h + 1) * 128], on[j][:, h * 128:(h + 1) * 128], ident)
                    return ins
                P.op("pe", f2, reads=[on_b[j]], writes=[TCbk])
                for h in range(4):
                    P.op("act", lambda e, h=h: e.activation(out=tce[:, h * 128:(h + 1) * 128], in_=tcb[:, h * 128:(h + 1) * 128], func=AF.Copy, scale=gng[:, h:h + 1]),
                         reads=[TCbk, gng_b], writes=[tce_b])
                P.op("pool", lambda e: e.tensor_tensor(out=rgs[:, :, j * 128:(j + 1) * 128], in0=v3(tce, 4), in1=RGg[gsl][:, :, j * 128:(j + 1) * 128], op=ALU.mult),
                     reads=[tce_b, RGg_b[gsl]], writes=[rgs_b[j]])

            def c_scores(gsl, h, kt, hp):
                hv = h % 2
                pr = h // 2
                sb_i = cc["S"] % 3
                cc["S"] += 1
                s_ap = bank(sb_i)

                def f(e):
                    e.matmul(s_ap, KTs[:, h, kt * 128:(kt + 1) * 128], QN[gsl][:, h, :], start=True, stop=False)
                    return e.matmul(s_ap, KPs[:, hv, kt * 128:(kt + 1) * 128], QP[gsl][:, pr, :], start=False, stop=True)
                P.op("pe", f, reads=[KT_b[h], KP_b, QN_b[gsl], QP_b[gsl]], writes=[Sbk[sb_i]])
                pi = cc["PT"] % NPT
                cc["PT"] += 1
                P.op("act", lambda e: e.activation(out=PT[pi], in_=s_ap, func=AF.Exp, scale=scale), reads=[Sbk[sb_i]], writes=[PT_b[pi]])
                return pi

            def c_pv(h, kt, nt, pi, az):
                a_ap, z_ap = bank(3 + az), bank(5 + az)
                st_, sp__ = (kt == 0), (kt == nt - 1)

                def f(e):
                    e.matmul(a_ap, Vsq[:, kt, h * 128:(h + 1) * 128], PT[pi], start=st_, stop=sp__)
                    return e.matmul(z_ap, ones, PT[pi], start=st_, stop=sp__)
                P.op("pe", f, reads=[V_b, PT_b[pi]], writes=[Abk[az], Zbk[az]])

            def c_head(gsl, h, nt, tok0, extras, pending):
                az = cc["AZ"] % 2
                cc["AZ"] += 1
                hp = az
                a_ap, z_ap = bank(3 + az), bank(5 + az)
                step = max(1, nt // 2)
                la = min(2, nt - 1)
                pis = [c_scores(gsl, h, k, hp) for k in range(la + 1)]
                for kt in range(nt):
                    if kt + la + 1 < nt:
                        pis.append(c_scores(gsl, h, kt + la + 1, hp))
                    c_pv(h, kt, nt, pis[kt], az)
                    if pending is not None and kt == min(1, nt - 1):
                        pending()
                        pending = None
                    if scan_q and kt % 4 == 3:
                        scan_q.pop(0)()
                    if extras and (kt % step) == step - 1:
                        extras.pop(0)()

                def epilogue():
                    P.op("dve", lambda e: e.reciprocal(out=rec, in_=z_ap), reads=[Zbk[az]], writes=[rec_b])
                    P.op("dve", lambda e: e.tensor_tensor(out=an, in0=a_ap, in1=rec, op=ALU.mult), reads=[Abk[az], rec_b], writes=[an_b])
                    ai = cc["ag"] % 2
                    cc["ag"] += 1
                    P.op("pool", lambda e: e.tensor_tensor(out=ags[ai], in0=an, in1=MG[gsl][:, h, :], op=ALU.mult), reads=[an_b, MG_b[gsl]], writes=[ags_b[ai]])
                    P.dma(AG[h, :, tok0:tok0 + 512], ags[ai], reads=[ags_b[ai]])
                return epilogue

            def c_gloads(gsl, tok0):
                P.dma(QN[gsl], QTN[:, :, tok0:tok0 + 512].rearrange("h p t -> p h t"), writes=[QN_b[gsl]])
                P.dma(QP[gsl], QTP[:, :, tok0:tok0 + 512].rearrange("q p t -> p q t"), writes=[QP_b[gsl]])
                P.dma(MG[gsl], MGT[:, :, tok0:tok0 + 512].rearrange("h p t -> p h t"), writes=[MG_b[gsl]])

            def c_gloads_ret(tok0):
                gsl = 0
                P.dma(QXf[gsl], QXF[:, :, tok0:tok0 + 512].rearrange("h p t -> p h t"), writes=[QX_b[gsl][0]])
                P.dma(QXb_[gsl], QXB[:, :, tok0:tok0 + 512].rearrange("h p t -> p h t"), writes=[QX_b[gsl][1]])
                tg0 = tok0 // 128
                P.dma(SFg[gsl], SF[tg0:tg0 + 4].rearrange("j p c -> p j c"), reads=SFd_b[tg0:tg0 + 4], writes=[SG_b[gsl][0]])
                P.dma(SBg[gsl], SB[tg0:tg0 + 4].rearrange("j p c -> p j c"), reads=SBd_b[tg0:tg0 + 4], writes=[SG_b[gsl][1]])
                P.dma(OIg[gsl], OI[tok0:tok0 + 512, :].rearrange("(j p) c -> p j c", p=128), writes=[OIg_b[gsl]])
                P.dma(RGg[gsl], RGT[:, :, tok0:tok0 + 512].rearrange("h p t -> p h t"), writes=[RGg_b[gsl]])

            def c_group(gidx, tok0, nt, next_tok0):
                gsl = gidx % 2
                extras = []
                if next_tok0 is not None:
                    extras.append(lambda: c_gloads(1 - gsl, next_tok0))
                extras += [lambda: c_ret1(gsl, 0),
                           lambda: c_ret1(gsl, 1),
                           lambda: (c_ret2(gsl, 0), c_ret1(gsl, 2)),
                           lambda: (c_ret2(gsl, 1), c_ret1(gsl, 3)),
                           lambda: c_ret2(gsl, 2),
                           lambda: c_ret2(gsl, 3)]
                if next_tok0 is not None:
                    extras.append(lambda: c_gloads_ret(next_tok0))
                for h in range(4):
                    cpend[0] = c_head(gsl, h, nt, tok0, extras, cpend[0])
                while extras:
                    extras.pop(0)()
                P.dma(RETG[:, :, tok0:tok0 + 512].rearrange("h p t -> p h t"), rgs, reads=rgs_b)

            cgl = [(si, seq_off[si] + gi * 512) for si, s in enumerate(seqs) for gi in range(s // 512)]
            for f_ in scan_list(0):
                f_()
            c_gloads(0, cgl[0][1])
            c_gloads_ret(cgl[0][1])
            for gidx, (si, tok0) in enumerate(cgl):
                s = seqs[si]
                s0 = seq_off[si]
                nt = s // 128
                if tok0 == s0:
                    P.dma(KTs[:, 0, 0:s], KTN[0, :, s0:s0 + s], writes=[KT_b[0]])
                    P.dma(KPs[:, :, 0:s], KTP[:, :, s0:s0 + s].rearrange("v p t -> p v t"), writes=[KP_b])
                    for j0 in range(0, nt, 8):
                        j1 = min(nt, j0 + 8)
                        P.dma(Vsq[:, j0:j1, :], VV[s0 + j0 * 128:s0 + j1 * 128, :].rearrange("(j p) c -> p j c", p=128), writes=[V_b])
                    for h in range(1, 4):
                        P.dma(KTs[:, h, 0:s], KTN[h, :, s0:s0 + s], writes=[KT_b[h]])
                    assert not scan_q
                    if si + 1 < len(seqs):
                        scan_q.extend(scan_list(si + 1))
                last_of_seq = (tok0 + 512 == s0 + s)
                if last_of_seq:
                    while scan_q:
                        scan_q.pop(0)()
                c_group(gidx, tok0, nt, cgl[gidx + 1][1] if gidx + 1 < len(cgl) else None)
            cpend[0]()
            cpend[0] = None
            P.barrier()
            if stop_after == "C":
                break

            A.release(base_mark)
            Wr = A.alloc(BF16, 4, D)
            Wm = A.alloc(BF16, 4, D)
            Wo = A.alloc(BF16, KC, D)
            gfin = A.alloc(F32, D)
            wD_mark = A.mark()
            stgD = [A.alloc(F32, 1024), A.alloc(F32, 1024)]
            stgD_b = bufs(2)

            def d_wload(src, dst, kc, k, use_act):
                P.dma(stgD[k], src[L, kc * 128:(kc + 1) * 128, :], writes=[stgD_b[k]])
                if use_act:
                    P.op("act", lambda e: e.activation(out=dst[:, kc, :], in_=stgD[k], func=AF.Copy), reads=[stgD_b[k]])
                else:
                    P.op("dve", lambda e: e.tensor_copy(out=dst[:, kc, :], in_=stgD[k]), reads=[stgD_b[k]])
            wi = 0
            for (src, dst, n) in ((W_brr, Wr, 4), (W_brm, Wm, 4), (W_out, Wo, 8)):
                for kc in range(n):
                    d_wload(src, dst, kc, wi % 2, wi % 2 == 0)
                    wi += 1
            if last:
                gf_b = Buf()
                P.dma(gfin, W_fin.partition_broadcast(128), writes=[gf_b])
                P.op("dve", lambda e: e.tensor_scalar(out=gfin, in0=gfin, scalar1=32.0, scalar2=None, op0=ALU.mult), reads=[gf_b], writes=[gf_b])
            P.barrier()
            A.release(wD_mark)
            RG2 = [A.alloc(BF16, 4, 512) for _ in range(2)]
            AG2 = [A.alloc(BF16, 4, 512) for _ in range(2)]
            GR2 = [A.alloc(BF16, 8, 512) for _ in range(2)]
            GM2 = [A.alloc(BF16, 8, 512) for _ in range(2)]
            in_b = [bufs(4), bufs(4)]
            mer = [A.alloc(BF16, 8, 512) for _ in range(2)]
            mer_b = [bufs(8), bufs(8)]
            NX = 8
            xr = [A.alloc(F32, D) for _ in range(NX)]
            xr_b = bufs(NX)
            t1 = [A.alloc(F32, 512) for _ in range(2)]
            t1_b = bufs(2)
            t2 = [A.alloc(F32, 512) for _ in range(2)]
            t2_b = bufs(2)
            jk = A.alloc(BF16, D)
            Pbk = bufs(8)
            cd = {"p": 0, "t": 0, "x": 0}
            Xres = Xin
            Xout = Y if last else XMID

            def pbank():
                i = cd["p"] % 8
                cd["p"] += 1
                return bank(i), Pbk[i]

            glistD = [(seq_off[si] + gi * 512) for si, s in enumerate(seqs) for gi in range(s // 512)]

            def d_loads(g):
                tok0 = glistD[g]
                k = g % 2
                P.dma(RG2[k], RETG[:, :, tok0:tok0 + 512].rearrange("h p t -> p h t"), writes=[in_b[k][0]])
                P.dma(AG2[k], AG[:, :, tok0:tok0 + 512].rearrange("h p t -> p h t"), writes=[in_b[k][1]])
                P.dma(GR2[k], GRT[:, :, tok0:tok0 + 512].rearrange("h p t -> p h t"), writes=[in_b[k][2]])
                P.dma(GM2[k], GMT[:, :, tok0:tok0 + 512].rearrange("h p t -> p h t"), writes=[in_b[k][3]])

            def d_merge(k, dc):
                pa, pab = pbank()
                pm_, pmb = pbank()

                def fa(e):
                    for h in range(4):
                        ins = e.matmul(pa, Wr[:, h, dc * 128:(dc + 1) * 128], RG2[k][:, h, :], start=(h == 0), stop=(h == 3))
                    return ins

                def fm(e):
                    for h in range(4):
                        ins = e.matmul(pm_, Wm[:, h, dc * 128:(dc + 1) * 128], AG2[k][:, h, :], start=(h == 0), stop=(h == 3))
                    return ins
                P.op("pe", fa, reads=[in_b[k][0]], writes=[pab])
                P.op("pe", fm, reads=[in_b[k][1]], writes=[pmb])
                ti = cd["t"] % 2
                cd["t"] += 1
                P.op("dve", lambda e: e.tensor_tensor(out=t1[ti], in0=pa, in1=GR2[k][:, dc, :], op=ALU.mult), reads=[pab, in_b[k][2]], writes=[t1_b[ti]])
                P.op("dve", lambda e: e.tensor_tensor(out=t2[ti], in0=pm_, in1=GM2[k][:, dc, :], op=ALU.mult), reads=[pmb, in_b[k][3]], writes=[t2_b[ti]])
                P.op("pool", lambda e: e.tensor_tensor(out=mer[k][:, dc, :], in0=t1[ti], in1=t2[ti], op=ALU.add), reads=[t1_b[ti], t2_b[ti]], writes=[mer_b[k][dc]])

            def d_out_half(k, j, hf, xi):
                po, pob = pbank()

                def fo(e):
                    for dc in range(8):
                        ins = e.matmul(po, mer[k][:, dc, j * 128:(j + 1) * 128], Wo[:, dc, hf * 512:(hf + 1) * 512], start=(dc == 0), stop=(dc == 7))
                    return ins
                P.op("pe", fo, reads=mer_b[k], writes=[pob])
                P.op("dve", lambda e: e.tensor_tensor(out=xr[xi][:, hf * 512:(hf + 1) * 512], in0=po, in1=xr[xi][:, hf * 512:(hf + 1) * 512], op=ALU.add),
                     reads=[pob, xr_b[xi]], writes=[xr_b[xi]])

            def d_xload(g):
                for j in range(4):
                    xi = (g * 4 + j) % NX
                    r0 = glistD[g] + j * 128
                    P.dma(xr[xi], Xres[r0:r0 + 128, :], writes=[xr_b[xi]])

            def d_tile(g, k, tok0, j):
                xi = (g * 4 + j) % NX
                r0 = tok0 + j * 128
                for hf in range(2):
                    d_out_half(k, j, hf, xi)
                if last:
                    ss, ssb = stat()
                    P.op("act", lambda e: e.activation(out=jk, in_=xr[xi], func=AF.Square, accum_out=ss), reads=[xr_b[xi]], writes=ssb)
                    rs, rsb = stat()
                    rstd_ops(ss, ssb, rs, rsb, 0)
                    P.op("dve", lambda e: e.scalar_tensor_tensor(out=xr[xi], in0=xr[xi], scalar=rs, in1=gfin, op0=ALU.mult, op1=ALU.mult),
                         reads=[xr_b[xi]] + rsb, writes=[xr_b[xi]])
                P.dma(Xout[r0:r0 + 128, :], xr[xi], reads=[xr_b[xi]])

            ND = len(glistD)
            d_loads(0)
            d_xload(0)
            for dc in range(8):
                d_merge(0, dc)
            for g in range(ND):
                if g + 1 < ND:
                    d_loads(g + 1)
                    d_xload(g + 1)
                for j in range(4):
                    d_tile(g, g % 2, glistD[g], j)
                    if g + 1 < ND:
                        d_merge((g + 1) % 2, 2 * j)
                        d_merge((g + 1) % 2, 2 * j + 1)
            P.barrier()

        P.barrier()
        P.emit()
        build_program.peak = A.peak
        build_program.P = P
    return nc


_CACHE = {}


def _get_program(seqs):
    key = tuple(seqs)
    if key not in _CACHE:
        _CACHE[key] = build_program(seqs)
    return _CACHE[key]


def kernel(x_prompt, x_sample, norm_g, w_in, ret_gn_g, q_norm_g, kv_norm_g, w_uq, w_ukv, w_br_ret, w_br_mla, w_out, final_norm_g):
    x_prompt = np.asarray(x_prompt, dtype=np.float32)
    x_sample = np.asarray(x_sample, dtype=np.float32)
    nb_p, s_p, _ = x_prompt.shape
    nb_s, s_s, _ = x_sample.shape
    pp = nb_p // N_CORES
    sp_ = nb_s // N_CORES
    seqs = tuple([s_p] * pp + [s_s] * sp_)
    nc = _get_program(seqs)
    consts = host_consts(max(seqs))
    wts = {
        "norm_g": np.ascontiguousarray(norm_g, dtype=np.float32), "w_in": np.ascontiguousarray(w_in, dtype=np.float32),
        "ret_gn_g": np.ascontiguousarray(ret_gn_g, dtype=np.float32), "q_norm_g": np.ascontiguousarray(q_norm_g, dtype=np.float32),
        "kv_norm_g": np.ascontiguousarray(kv_norm_g, dtype=np.float32), "w_uq": np.ascontiguousarray(w_uq, dtype=np.float32),
        "w_ukv": np.ascontiguousarray(w_ukv, dtype=np.float32), "w_br_ret": np.ascontiguousarray(w_br_ret, dtype=np.float32),
        "w_br_mla": np.ascontiguousarray(w_br_mla, dtype=np.float32), "w_out": np.ascontiguousarray(w_out, dtype=np.float32),
        "final_norm_g": np.ascontiguousarray(final_norm_g, dtype=np.float32),
    }
    in_maps = []
    for c in range(N_CORES):
        xc = np.concatenate([x_prompt[c * pp:(c + 1) * pp].reshape(pp * s_p, D), x_sample[c * sp_:(c + 1) * sp_].reshape(sp_ * s_s, D)], axis=0)
        m = {"x": xc}
        m.update(wts)
        m.update(consts)
        in_maps.append(m)
    res = run_bass_kernel_spmd(nc, in_maps, core_ids=list(range(N_CORES)))
    y_p = np.empty_like(x_prompt)
    y_s = np.empty_like(x_sample)
    for c in range(N_CORES):
        yc = np.asarray(res.results[c]["y"], dtype=np.float32)
        y_p[c * pp:(c + 1) * pp] = yc[:pp * s_p].reshape(pp, s_p, D)
        y_s[c * sp_:(c + 1) * sp_] = yc[pp * s_p:].reshape(sp_, s_s, D)
    return (y_p, y_s)
```

```python
import numpy as np
import ml_dtypes
from contextlib import ExitStack
import concourse.bass as bass
import concourse.mybir as mybir
from concourse.bass_utils import run_bass_kernel_spmd

F32 = mybir.dt.float32
BF16 = mybir.dt.bfloat16
AF = mybir.ActivationFunctionType
ALU = mybir.AluOpType
AX = mybir.AxisListType

D = 1024
KC = 8
INW = 5056
C_RQ, C_RK, C_RV, C_RG, C_CQ, C_CKV, C_KPE, C_MG, C_GR, C_GM = 0, 512, 1024, 1536, 2048, 2304, 2432, 2496, 3008, 4032
EPS = 1e-6
RET_LOG2_FWD = (-5.0, -6.0, -7.0, -8.0)
RET_LOG2_BWD = (-5.5, -6.5, -7.5, -8.5)
ARENA = 212736
N_CORES = 8
SEQS_FULL = (2048, 2048, 2048, 2048, 4096)


def host_consts(smax):
    bf = ml_dtypes.bfloat16
    c = {}
    c["c_ident"] = np.eye(128, dtype=np.float32).astype(bf)
    c["c_ones"] = np.ones((128, 128), np.float32).astype(bf)
    pos = np.arange(smax, dtype=np.float32)
    inv_r = (np.float32(10000.0) ** (-(np.arange(0, 128, 2, dtype=np.float32) / np.float32(128)))).astype(np.float32)
    ang_r = (pos[:, None] * inv_r[None, :]).astype(np.float32).astype(np.float64)
    c["c_cosr"] = np.cos(ang_r).astype(np.float32)
    c["c_sinr"] = np.sin(ang_r).astype(np.float32)
    inv_m = (np.float32(10000.0) ** (-(np.arange(0, 64, 2, dtype=np.float32) / np.float32(64)))).astype(np.float32)
    ang_m = (pos[:, None] * inv_m[None, :]).astype(np.float32).astype(np.float64)
    idx = (np.arange(128) % 64) % 32
    c["c_cosm"] = np.ascontiguousarray(np.cos(ang_m)[:, idx].T).astype(np.float32)
    c["c_sinm"] = np.ascontiguousarray(np.sin(ang_m)[:, idx].T).astype(np.float32)
    lgf = np.log1p(-np.exp2(np.array(RET_LOG2_FWD, np.float64)))
    lgb = np.log1p(-np.exp2(np.array(RET_LOG2_BWD, np.float64)))
    k = np.arange(128)[:, None, None].astype(np.float64)
    q = np.arange(128)[None, None, :].astype(np.float64)
    diff = q - k
    M = np.where(diff >= 0, np.exp(lgf[None, :, None] * np.maximum(diff, 0)), np.exp(lgb[None, :, None] * np.maximum(-diff, 0)))
    c["c_mask"] = M.reshape(128, 512).astype(np.float32)
    qq = np.arange(128, dtype=np.float64)
    xif = np.exp(lgf[:, None] * (qq[None, :] + 1.0)).reshape(512)
    xib = np.exp(lgb[:, None] * (128.0 - qq[None, :])).reshape(512)
    decf = np.repeat(np.exp(lgf * 128.0), 128)
    decb = np.repeat(np.exp(lgb * 128.0), 128)
    c["c_rows"] = np.stack([xif, xib, decf, decb]).astype(np.float32)
    zf = np.exp(lgf[None, :] * (127.0 - qq[:, None]))
    zb = np.exp(lgb[None, :] * qq[:, None])
    c["c_zeta"] = np.concatenate([zf, zb], axis=1).astype(np.float32)
    return c


class Buf:
    __slots__ = ("w", "rd")

    def __init__(self):
        self.w = None
        self.rd = {}


def bufs(n):
    return [Buf() for _ in range(n)]


class Eng:
    def __init__(self, key, sem):
        self.key = key
        self.sem = sem
        self.n = 0
        self.ops = []
        self.known = {}


class Prog:
    def __init__(self, nc, stack, n_dma_sems=24):
        self.nc = nc
        self.sems = []
        self.E = {}
        for key in ("pe", "act", "dve", "pool", "sp"):
            s = stack.enter_context(nc.semaphore("s_" + key))
            self.sems.append(s)
            self.E[key] = Eng(key, len(self.sems) - 1)
        self.dq = {}
        for key in ("sp", "pool"):
            ids = []
            for i in range(n_dma_sems):
                s = stack.enter_context(nc.semaphore("d_%s%d" % (key, i)))
                self.sems.append(s)
                ids.append(len(self.sems) - 1)
            self.dq[key] = {"ids": ids, "vals": [0] * n_dma_sems, "next": 0}

    def _need(self, reads, writes):
        need = {}
        for b in reads:
            if b.w is not None and need.get(b.w[0], 0) < b.w[1]:
                need[b.w[0]] = b.w[1]
        for b in writes:
            if b.w is not None and need.get(b.w[0], 0) < b.w[1]:
                need[b.w[0]] = b.w[1]
            for s, v in b.rd.items():
                if need.get(s, 0) < v:
                    need[s] = v
        return need

    def _record(self, tok, reads, writes):
        for b in reads:
            if b.rd.get(tok[0], 0) < tok[1]:
                b.rd[tok[0]] = tok[1]
        for b in writes:
            b.w = tok
            b.rd = {}

    def op(self, ek, fn, reads=(), writes=()):
        eng = self.E[ek]
        need = self._need(reads, writes)
        if ek == "pe":
            need.pop(eng.sem, None)
        waits = []
        for s, v in need.items():
            if eng.known.get(s, 0) < v:
                eng.known[s] = v
                waits.append((s, v))
        eng.n += 1
        tok = (eng.sem, eng.n)
        eng.ops.append((waits, fn, (eng.sem, 1)))
        self._record(tok, reads, writes)
        return tok

    def dma(self, out, in_, reads=(), writes=(), q="sp", slow=False):
        eng = self.E[q]
        dq = self.dq[q]
        j = dq["next"] % len(dq["ids"])
        dq["next"] += 1
        sid = dq["ids"][j]
        need = self._need(reads, writes)
        prev = dq["vals"][j]
        if prev and need.get(sid, 0) < prev:
            need[sid] = prev
        waits = []
        for s, v in need.items():
            if eng.known.get(s, 0) < v:
                eng.known[s] = v
                waits.append((s, v))
        dq["vals"][j] = prev + 16
        tok = (sid, prev + 16)
        if slow:
            eng.ops.append((waits, (lambda e, o=out, i=in_: e.dma_start(out=o, in_=i, allow_slow_non_contiguous=True)), (sid, 16)))
        else:
            eng.ops.append((waits, (lambda e, o=out, i=in_: e.dma_start(out=o, in_=i)), (sid, 16)))
        self._record(tok, reads, writes)
        return tok

    def barrier(self):
        toks = {}
        for e in self.E.values():
            if e.n:
                toks[e.sem] = e.n
        for dq in self.dq.values():
            for sid, v in zip(dq["ids"], dq["vals"]):
                if v:
                    toks[sid] = v
        for e in self.E.values():
            waits = []
            for s, v in toks.items():
                if e.known.get(s, 0) < v:
                    e.known[s] = v
                    waits.append((s, v))
            if waits:
                e.ops.append((waits, None, None))

    def emit(self):
        sems = self.sems
        with self.nc.Block() as block:
            for key, deco in (("pe", block.tensor), ("act", block.scalar), ("dve", block.vector),
                              ("pool", block.gpsimd), ("sp", block.sync)):
                ops = self.E[key].ops

                def body(e, ops=ops):
                    for waits, fn, inc in ops:
                        for s, v in waits:
                            e.wait_ge(sems[s], v)
                        if fn is not None:
                            ins = fn(e)
                            ins.then_inc(sems[inc[0]], inc[1])
                deco(body)


class Arena:
    def __init__(self, t, nbytes):
        self.t = t
        self.n = nbytes
        self.off = 0
        self.peak = 0

    def mark(self):
        return self.off

    def release(self, m):
        self.off = m

    def alloc(self, dtype, *free):
        n = 1
        for f in free:
            n *= f
        nb = n * (4 if dtype == F32 else 2)
        off = (self.off + 63) // 64 * 64
        self.off = off + nb
        self.peak = max(self.peak, self.off)
        assert self.off <= self.n, "SBUF arena overflow: %d > %d" % (self.off, self.n)
        a = self.t[:, off // 2:(off + nb) // 2]
        if dtype == F32:
            a = a.bitcast(F32)
        if len(free) == 2:
            a = a.rearrange("p (a b) -> p a b", a=free[0])
        elif len(free) == 3:
            a = a.rearrange("p (a b c) -> p a b c", a=free[0], b=free[1])
        return a


def v3(ap, a):
    return ap.rearrange("p (a b) -> p a b", a=a)


def v4(ap, a, b):
    return ap.rearrange("p (a b c) -> p a b c", a=a, b=b)


def build_program(seqs, n_layers=2, debug=False, stop_after=None):
    T = sum(seqs)
    NTt = T // 128
    SMAX = max(seqs)
    assert all(s % 512 == 0 for s in seqs)
    nc = bass.Bass("TRN2", target_bir_lowering=False)
    dt = nc.dram_tensor

    def din(name, shape, dtype=F32):
        return dt(name, list(shape), dtype, kind="ExternalInput").ap()

    def scr(name, shape, dtype):
        return dt(name, list(shape), dtype, kind=("ExternalOutput" if debug else "Internal")).ap()

    X = din("x", [T, D])
    Y = dt("y", [T, D], F32, kind="ExternalOutput").ap()
    W_norm = din("norm_g", [2, D])
    W_in = din("w_in", [2, D, INW])
    W_gn = din("ret_gn_g", [2, 512])
    W_qn = din("q_norm_g", [2, 256])
    W_kvn = din("kv_norm_g", [2, 128])
    W_uq = din("w_uq", [2, 256, 768])
    W_ukv = din("w_ukv", [2, 128, 1024])
    W_brr = din("w_br_ret", [2, 512, D])
    W_brm = din("w_br_mla", [2, 512, D])
    W_out = din("w_out", [2, D, D])
    W_fin = din("final_norm_g", [D])
    Cd = {
        "ident": din("c_ident", [128, 128], BF16), "ones": din("c_ones", [128, 128], BF16),
        "cosr": din("c_cosr", [SMAX, 64]), "sinr": din("c_sinr", [SMAX, 64]),
        "cosm": din("c_cosm", [128, SMAX]), "sinm": din("c_sinm", [128, SMAX]),
        "mask": din("c_mask", [128, 512]), "rows": din("c_rows", [4, 512]), "zeta": din("c_zeta", [128, 8]),
    }
    XMID = scr("xmid", [T, D], F32)
    QTN = scr("s_qtn", [4, 128, T], BF16)
    QTP = scr("s_qtp", [2, 128, T], BF16)
    KTN = scr("s_ktn", [4, 128, T], BF16)
    KTP = scr("s_ktp", [2, 128, T], BF16)
    VV = scr("s_v", [T, 512], BF16)
    QXF = scr("s_qxf", [4, 128, T], BF16)
    QXB = scr("s_qxb", [4, 128, T], BF16)
    OI = scr("s_oi", [T, 512], F32)
    UF = scr("s_uf", [NTt, 128, 512], F32)
    UB = scr("s_ub", [NTt, 128, 512], F32)
    SF = scr("s_sf", [NTt, 128, 512], BF16)
    SB = scr("s_sb", [NTt, 128, 512], BF16)
    RGT = scr("s_rgt", [4, 128, T], BF16)
    MGT = scr("s_mgt", [4, 128, T], BF16)
    GRT = scr("s_grt", [8, 128, T], BF16)
    GMT = scr("s_gmt", [8, 128, T], BF16)
    RETG = scr("s_retg", [4, 128, T], BF16)
    AG = scr("s_ag", [4, 128, T], BF16)

    with ExitStack() as stack:
        arena_t = stack.enter_context(nc.sbuf_tensor("arena", [128, ARENA // 2], BF16))
        ps_t = stack.enter_context(nc.psum_tensor("ps", [128, 8, 512], F32))
        P = Prog(nc, stack)
        A = Arena(arena_t, ARENA)

        def bank(i):
            return ps_t[:, i, :]

        def bankb(i):
            return ps_t[:, i, :].bitcast(BF16)

        ident = A.alloc(BF16, 128)
        ones = A.alloc(BF16, 128)
        zeta = A.alloc(F32, 8)
        cst = A.alloc(F32, 8)
        stats = A.alloc(F32, 64)
        stat_bufs = bufs(64)
        stat_ctr = [0]

        def stat(n=1):
            i = stat_ctr[0]
            if (i % 64) + n > 64:
                i = (i // 64 + 1) * 64
            stat_ctr[0] = i + n
            i %= 64
            return stats[:, i:i + n], stat_bufs[i:i + n]

        P.dma(ident, Cd["ident"][:, :])
        P.dma(ones, Cd["ones"][:, :])
        P.dma(zeta, Cd["zeta"][:, :])
        P.op("pool", lambda e: e.memset(cst[:, 0:1], D * EPS))
        P.op("pool", lambda e: e.memset(cst[:, 1:2], 256 * EPS))
        P.op("pool", lambda e: e.memset(cst[:, 2:3], 128 * EPS))
        P.op("pool", lambda e: e.memset(cst[:, 3:6], -0.5))
        P.op("pool", lambda e: e.memset(cst[:, 6:7], 128 * EPS))
        P.barrier()
        base_mark = A.mark()

        def rstd_ops(ss_ap, ss_b, out_ap, out_b, c0, n=1):
            tmp, tmp_b = stat(n)
            P.op("pool", lambda e: e.tensor_tensor(out=tmp, in0=ss_ap, in1=cst[:, c0:c0 + n], op=ALU.add),
                 reads=ss_b, writes=tmp_b)
            P.op("pool", lambda e: e.tensor_tensor(out=out_ap, in0=tmp, in1=cst[:, 3:3 + n], op=ALU.pow),
                 reads=tmp_b, writes=out_b)

        seq_off = []
        o = 0
        for s in seqs:
            seq_off.append(o)
            o += s

        for L in range(n_layers):
            Xin = X if L == 0 else XMID
            last = (L == n_layers - 1)
            A.release(base_mark)
            Wg = A.alloc(BF16, KC, INW)
            Wkpe = A.alloc(BF16, KC, 128)
            Wkpr = A.alloc(BF16, KC, 128)
            Wuq = A.alloc(BF16, 2, 768)
            Wqpe = A.alloc(BF16, 2, 2, 128)
            Wqpr = A.alloc(BF16, 2, 2, 128)
            Wukv = A.alloc(BF16, 1024)
            Wv = A.alloc(BF16, 512)
            mask = A.alloc(F32, 512)
            xif = A.alloc(F32, 512)
            xib = A.alloc(F32, 512)
            P.dma(mask, Cd["mask"][:, :])
            P.dma(xif, Cd["rows"][0].partition_broadcast(128))
            P.dma(xib, Cd["rows"][1].partition_broadcast(128))
            gcol = A.alloc(F32, 8)
            qgcol = A.alloc(F32, 2)
            kvgcol = A.alloc(F32, 1)
            wA_mark = A.mark()
            stg = [A.alloc(F32, 2528), A.alloc(F32, 2528)]
            stg_b = bufs(2)
            gcol_b, qg_b, kvg_b = Buf(), Buf(), Buf()
            P.dma(gcol, W_norm[L].rearrange("(c p) -> p c", p=128), writes=[gcol_b], slow=True)
            P.dma(qgcol, W_qn[L].rearrange("(c p) -> p c", p=128), writes=[qg_b], slow=True)
            P.dma(kvgcol, W_kvn[L].rearrange("(c p) -> p c", p=128), writes=[kvg_b], slow=True)
            P.op("dve", lambda e: e.tensor_scalar(out=gcol, in0=gcol, scalar1=32.0, scalar2=None, op0=ALU.mult),
                 reads=[gcol_b], writes=[gcol_b])
            P.op("dve", lambda e: e.tensor_scalar(out=qgcol, in0=qgcol, scalar1=16.0, scalar2=None, op0=ALU.mult),
                 reads=[qg_b], writes=[qg_b])
            P.op("dve", lambda e: e.tensor_scalar(out=kvgcol, in0=kvgcol, scalar1=float(np.sqrt(128.0)), scalar2=None, op0=ALU.mult),
                 reads=[kvg_b], writes=[kvg_b])
            wi = 0
            for kc in range(KC):
                for half in range(2):
                    sb = stg_b[wi % 2]
                    st_ap = stg[wi % 2]
                    c0 = half * 2528
                    P.dma(st_ap, W_in[L, kc * 128:(kc + 1) * 128, c0:c0 + 2528], writes=[sb])
                    if half == 0:
                        P.op("act", lambda e, kc=kc, st_ap=st_ap: e.activation(out=Wg[:, kc, 0:512], in_=st_ap[:, 0:512], func=AF.Copy, scale=gcol[:, kc:kc + 1]),
                             reads=[sb, gcol_b])
                        P.op("dve", lambda e, kc=kc, st_ap=st_ap: e.tensor_scalar(out=Wg[:, kc, 512:1024], in0=st_ap[:, 512:1024], scalar1=gcol[:, kc:kc + 1], scalar2=float(128.0 ** -0.5), op0=ALU.mult, op1=ALU.mult),
                             reads=[sb, gcol_b])
                        P.op("act", lambda e, kc=kc, st_ap=st_ap: e.activation(out=Wg[:, kc, 1024:2528], in_=st_ap[:, 1024:2528], func=AF.Copy, scale=gcol[:, kc:kc + 1]),
                             reads=[sb, gcol_b])
                    else:
                        P.op("dve", lambda e, kc=kc, st_ap=st_ap: e.tensor_scalar(out=Wg[:, kc, 2528:3792], in0=st_ap[:, 0:1264], scalar1=gcol[:, kc:kc + 1], scalar2=None, op0=ALU.mult),
                             reads=[sb, gcol_b])
                        P.op("act", lambda e, kc=kc, st_ap=st_ap: e.activation(out=Wg[:, kc, 3792:5056], in_=st_ap[:, 1264:2528], func=AF.Copy, scale=gcol[:, kc:kc + 1]),
                             reads=[sb, gcol_b])
                    wi += 1
            P.barrier()
            kp = Wg[:, :, C_KPE:C_KPE + 64]
            for hf in range(2):
                P.op("dve", lambda e, hf=hf: e.tensor_copy(out=Wkpe[:, :, hf * 64:hf * 64 + 64], in_=kp))
                P.op("dve", lambda e, hf=hf: e.tensor_scalar(out=Wkpr[:, :, hf * 64:hf * 64 + 32], in0=Wg[:, :, C_KPE + 32:C_KPE + 64], scalar1=-1.0, scalar2=None, op0=ALU.mult))
                P.op("dve", lambda e, hf=hf: e.tensor_copy(out=Wkpr[:, :, hf * 64 + 32:hf * 64 + 64], in_=Wg[:, :, C_KPE:C_KPE + 32]))
            for kc in range(2):
                sb = stg_b[wi % 2]
                st_ap = stg[wi % 2]
                wi += 1
                P.dma(st_ap[:, 0:768], W_uq[L, kc * 128:(kc + 1) * 128, :], writes=[sb])
                P.op("act", lambda e, kc=kc, st_ap=st_ap: e.activation(out=Wuq[:, kc, :], in_=st_ap[:, 0:768], func=AF.Copy, scale=qgcol[:, kc:kc + 1]),
                     reads=[sb, qg_b], writes=[sb])
            P.barrier()
            for kc in range(2):
                for h in range(4):
                    pr, hh = h // 2, h % 2
                    src = Wuq[:, kc, h * 192 + 128:h * 192 + 192]
                    P.op("dve", lambda e, kc=kc, pr=pr, hh=hh, src=src: e.tensor_copy(out=Wqpe[:, kc, pr, hh * 64:hh * 64 + 64], in_=src))
                    P.op("dve", lambda e, kc=kc, pr=pr, hh=hh, src=src: e.tensor_scalar(out=Wqpr[:, kc, pr, hh * 64:hh * 64 + 32], in0=src[:, 32:64], scalar1=-1.0, scalar2=None, op0=ALU.mult))
                    P.op("dve", lambda e, kc=kc, pr=pr, hh=hh, src=src: e.tensor_copy(out=Wqpr[:, kc, pr, hh * 64 + 32:hh * 64 + 64], in_=src[:, 0:32]))
            sb = stg_b[wi % 2]
            st_ap = stg[wi % 2]
            wi += 1
            P.dma(st_ap[:, 0:1024], W_ukv[L, :, :], writes=[sb])
            P.op("act", lambda e, st_ap=st_ap: e.activation(out=Wukv, in_=st_ap[:, 0:1024], func=AF.Copy, scale=kvgcol[:, 0:1]),
                 reads=[sb, kvg_b], writes=[sb])
            P.barrier()
            P.op("dve", lambda e: e.tensor_copy(out=v3(Wv, 4), in_=v4(Wukv, 4, 2)[:, :, 1, :]))
            P.barrier()
            A.release(wA_mark)
            if stop_after == "Aw":
                break

            xs = [A.alloc(F32, 1024) for _ in range(2)]
            xs_b = bufs(2)
            hb = [A.alloc(BF16, 1024) for _ in range(2)]
            hb_b = bufs(2)
            hT = [A.alloc(BF16, KC, 512) for _ in range(2)]
            hT_b = [bufs(4), bufs(4)]
            tabr = [(A.alloc(F32, 4, 64), A.alloc(F32, 4, 64)) for _ in range(2)]
            tabr_b = [bufs(2), bufs(2)]
            tabm = [(A.alloc(F32, 512), A.alloc(F32, 512)) for _ in range(2)]
            tabm_b = [bufs(2), bufs(2)]
            qkr = [A.alloc(BF16, 1024) for _ in range(2)]
            qkr_b = [bufs(2), bufs(2)]
            rt = [A.alloc(F32, 4, 256) for _ in range(2)]
            rt_b = [bufs(4), bufs(4)]
            vb = [A.alloc(BF16, 512) for _ in range(2)]
            vb_b = bufs(2)
            cn = [A.alloc(BF16, 384) for _ in range(2)]
            cn_b = [bufs(2), bufs(2)]
            QTt = [A.alloc(BF16, 512) for _ in range(2)]
            QTt_b = bufs(2)
            KTt = [A.alloc(BF16, 512) for _ in range(2)]
            KTt_b = bufs(2)
            Kz = [A.alloc(BF16, 2, 512) for _ in range(2)]
            Kz_b = [bufs(2), bufs(2)]
            Pm = [A.alloc(BF16, 512) for _ in range(2)]
            Pm_b = bufs(2)
            OIs = [A.alloc(F32, 512) for _ in range(2)]
            OIs_b = bufs(2)
            Us_1 = A.alloc(F32, 2, 512)
            Us = [Us_1, Us_1]
            Us_1b = bufs(2)
            Us_b = [Us_1b, Us_1b]
            qxs = (A.alloc(BF16, 4, 512), A.alloc(BF16, 4, 512))
            qxs_b = [bufs(4), bufs(4)]
            cqT = A.alloc(BF16, 2, 512)
            cqT_b = bufs(4)
            ckvT = A.alloc(BF16, 512)
            ckvT_b = bufs(4)
            Vs = A.alloc(BF16, 4, 512)
            Vs_b = bufs(4)
            QTPs = A.alloc(BF16, 2, 512)
            QTPs_b = bufs(2)
            KTPs = A.alloc(BF16, 2, 512)
            KTPs_b = bufs(1)
            P.op("pool", lambda e: e.memset(KTPs, 0.0), writes=KTPs_b)
            NSLOT = 5
            slots = [A.alloc(BF16, 512) for _ in range(NSLOT)]
            slots_b = bufs(NSLOT)
            sgt = [A.alloc(F32, 512) for _ in range(2)]
            sgt_b = bufs(2)
            mrt = [A.alloc(F32, 512) for _ in range(2)]
            mrt_b = bufs(2)
            Tb = bufs(2)
            Zb = bufs(4)
            Rb = bufs(2)
            ctr = {"T": 0, "Z": 0, "R": 0, "slot": 0, "sg": 0, "mr": 0, "tile": 0}

            def nxt(k, n):
                i = ctr[k] % n
                ctr[k] += 1
                return i

            def zbank():
                i = nxt("Z", 4)
                return bank(2 + i), Zb[i]

            def tbank():
                i = nxt("T", 2)
                return bankb(i), Tb[i]

            def rbank():
                i = nxt("R", 2)
                return bank(6 + i), Rb[i]

            def slot():
                i = nxt("slot", NSLOT)
                return slots[i], slots_b[i]

            glist = []
            for si, s in enumerate(seqs):
                for gi in range(s // 512):
                    glist.append((si, seq_off[si] + gi * 512, gi * 512))
            NG = len(glist)

            def s0_load(g, j):
                _, tok0, pos0 = glist[g]
                ti = g * 4 + j
                sl = ti % 2
                if j == 0:
                    tr, trb = tabr[g % 2], tabr_b[g % 2]
                    tm, tmb = tabm[g % 2], tabm_b[g % 2]
                    P.dma(tr[0], Cd["cosr"][pos0:pos0 + 512, :].rearrange("(j p) i -> p j i", p=128), writes=[trb[0]])
                    P.dma(tr[1], Cd["sinr"][pos0:pos0 + 512, :].rearrange("(j p) i -> p j i", p=128), writes=[trb[1]])
                    P.dma(tm[0], Cd["cosm"][:, pos0:pos0 + 512], writes=[tmb[0]])
                    P.dma(tm[1], Cd["sinm"][:, pos0:pos0 + 512], writes=[tmb[1]])
                P.dma(xs[sl], Xin[tok0 + j * 128:tok0 + (j + 1) * 128, :], writes=[xs_b[sl]])

            def s0_norm(g, j):
                ti = g * 4 + j
                sl = ti % 2
                ss, ssb = stat()
                P.op("act", lambda e: e.activation(out=hb[sl], in_=xs[sl], func=AF.Square, accum_out=ss),
                     reads=[xs_b[sl]], writes=ssb + [hb_b[sl]])
                rs, rsb = stat()
                rstd_ops(ss, ssb, rs, rsb, 0)
                P.op("act", lambda e: e.activation(out=hb[sl], in_=xs[sl], func=AF.Copy, scale=rs),
                     reads=[xs_b[sl]] + rsb, writes=[hb_b[sl]])

            def s0_pre(g, j):
                s0_load(g, j)
                s0_norm(g, j)

            def s0_pe(g, j):
                ti = g * 4 + j
                sl = ti % 2
                tb_ap, tbb = tbank()

                def f(e):
                    for kc in range(KC):
                        ins = e.transpose(tb_ap[:, kc * 128:(kc + 1) * 128], hb[sl][:, kc * 128:(kc + 1) * 128], ident)
                    return ins
                P.op("pe", f, reads=[hb_b[sl]], writes=[tbb])
                dst = hT[g % 2][:, :, j * 128:(j + 1) * 128]
                P.op("act", lambda e: e.activation(out=dst, in_=v3(tb_ap, KC), func=AF.Copy),
                     reads=[tbb], writes=[hT_b[g % 2][j]])

            def mm8_fm(g, c0, m, zb_ap, wsrc=None):
                hTg = hT[g % 2]

                def f(e):
                    for kc in range(KC):
                        lhs = (Wg[:, kc, c0:c0 + m] if wsrc is None else wsrc[:, kc, :])
                        ins = e.matmul(zb_ap[0:m, :], lhs, hTg[:, kc, :], start=(kc == 0), stop=(kc == KC - 1))
                    return ins
                return f

            def mm8_tm(g, j, c0, n, zb_ap):
                hTg = hT[g % 2]

                def f(e):
                    for kc in range(KC):
                        ins = e.matmul(zb_ap[:, 0:n], hTg[:, kc, j * 128:(j + 1) * 128], Wg[:, kc, c0:c0 + n], start=(kc == 0), stop=(kc == KC - 1))
                    return ins
                return f

            def fm_gate(g, kind, idx):
                _, tok0, _ = glist[g]
                c0 = {"rg": C_RG, "mg": C_MG, "gr": C_GR, "gm": C_GM}[kind] + idx * 128
                dst = {"rg": RGT, "mg": MGT, "gr": GRT, "gm": GMT}[kind][idx, :, tok0:tok0 + 512]
                zb_ap, zbb = zbank()
                P.op("pe", mm8_fm(g, c0, 128, zb_ap), reads=hT_b[g % 2], writes=[zbb])
                sl_ap, slb = slot()
                if kind in ("gr", "gm"):
                    P.op("act", lambda e: e.activation(out=sl_ap, in_=zb_ap, func=AF.Sigmoid), reads=[zbb], writes=[slb])
                else:
                    i = nxt("sg", 2)
                    P.op("act", lambda e: e.activation(out=sgt[i], in_=zb_ap, func=AF.Sigmoid), reads=[zbb], writes=[sgt_b[i]])
                    P.op("dve", lambda e: e.tensor_tensor(out=sl_ap, in0=zb_ap, in1=sgt[i], op=ALU.mult),
                         reads=[zbb, sgt_b[i]], writes=[slb])
                P.dma(dst, sl_ap, reads=[slb])

            def fm_kpe(g):
                _, tok0, _ = glist[g]
                za, zab = zbank()
                P.op("pe", mm8_fm(g, 0, 128, za, wsrc=Wkpe), reads=hT_b[g % 2], writes=[zab])
                zr, zrb = zbank()
                P.op("pe", mm8_fm(g, 0, 128, zr, wsrc=Wkpr), reads=hT_b[g % 2], writes=[zrb])
                tm, tmb = tabm[g % 2], tabm_b[g % 2]
                i0, i1 = nxt("mr", 2), nxt("mr", 2)
                P.op("dve", lambda e: e.tensor_tensor(out=mrt[i0], in0=za, in1=tm[0], op=ALU.mult), reads=[zab, tmb[0]], writes=[mrt_b[i0]])
                P.op("dve", lambda e: e.tensor_tensor(out=mrt[i1], in0=zr, in1=tm[1], op=ALU.mult), reads=[zrb, tmb[1]], writes=[mrt_b[i1]])
                P.op("pool", lambda e: e.tensor_tensor(out=KTPs[0:64, 0, :], in0=mrt[i0][0:64, :], in1=mrt[i1][0:64, :], op=ALU.add),
                     reads=[mrt_b[i0], mrt_b[i1]], writes=KTPs_b)
                P.op("pool", lambda e: e.tensor_tensor(out=KTPs[64:128, 1, :], in0=mrt[i0][64:128, :], in1=mrt[i1][64:128, :], op=ALU.add),
                     reads=[mrt_b[i0], mrt_b[i1]], writes=KTPs_b)
                P.dma(KTP[:, :, tok0:tok0 + 512].rearrange("v p t -> p v t"), KTPs, reads=KTPs_b)

            def tm_rope(g, j, which):
                sl = (g * 4 + j) % 2
                zb_ap, zbb = zbank()
                P.op("pe", mm8_tm(g, j, (C_RQ, C_RK)[which], 512, zb_ap), reads=[hT_b[g % 2][j]], writes=[zbb])
                zv = v4(zb_ap, 4, 2)
                tr, trb = tabr[g % 2], tabr_b[g % 2]
                cosb = tr[0][:, j, :].unsqueeze(1).to_broadcast([128, 4, 64])
                sinb = tr[1][:, j, :].unsqueeze(1).to_broadcast([128, 4, 64])
                t = [v3(rt[sl][:, i, :], 4) for i in range(4)]
                tb_ = rt_b[sl]
                outv = v4(qkr[sl][:, which * 512:(which + 1) * 512], 4, 2)
                ob = qkr_b[sl][which]
                P.op("dve", lambda e: e.tensor_tensor(out=t[0], in0=zv[:, :, 0, :], in1=cosb, op=ALU.mult), reads=[zbb, trb[0]], writes=[tb_[0]])
                P.op("dve", lambda e: e.tensor_tensor(out=t[1], in0=zv[:, :, 1, :], in1=sinb, op=ALU.mult), reads=[zbb, trb[1]], writes=[tb_[1]])
                P.op("dve", lambda e: e.tensor_tensor(out=t[2], in0=zv[:, :, 1, :], in1=cosb, op=ALU.mult), reads=[zbb, trb[0]], writes=[tb_[2]])
                P.op("dve", lambda e: e.tensor_tensor(out=t[3], in0=zv[:, :, 0, :], in1=sinb, op=ALU.mult), reads=[zbb, trb[1]], writes=[tb_[3]])
                P.op("pool", lambda e: e.tensor_tensor(out=outv[:, :, 0, :], in0=t[0], in1=t[1], op=ALU.subtract), reads=[tb_[0], tb_[1]], writes=[ob])
                P.op("pool", lambda e: e.tensor_tensor(out=outv[:, :, 1, :], in0=t[2], in1=t[3], op=ALU.add), reads=[tb_[2], tb_[3]], writes=[ob])
                if which == 1:
                    kv_ = v3(qkr[sl][:, 512:1024], 4)
                    for d_ in range(2):
                        zt = zeta[:, d_ * 4:d_ * 4 + 4].unsqueeze(2).to_broadcast([128, 4, 128])
                        P.op("pool", lambda e, d_=d_, zt=zt: e.tensor_tensor(out=v3(Kz[sl][:, d_, :], 4), in0=kv_, in1=zt, op=ALU.mult),
                             reads=[ob], writes=[Kz_b[sl][d_]])

            def tm_v(g, j):
                sl = (g * 4 + j) % 2
                zb_ap, zbb = zbank()
                P.op("pe", mm8_tm(g, j, C_RV, 512, zb_ap), reads=[hT_b[g % 2][j]], writes=[zbb])
                P.op("act", lambda e: e.activation(out=vb[sl], in_=zb_ap, func=AF.Copy), reads=[zbb], writes=[vb_b[sl]])

            def tm_c(g, j):
                sl = (g * 4 + j) % 2
                zb_ap, zbb = zbank()
                P.op("pe", mm8_tm(g, j, C_CQ, 384, zb_ap), reads=[hT_b[g % 2][j]], writes=[zbb])
                ss, ssb = stat(2)
                P.op("act", lambda e: e.activation(out=cn[sl][:, 0:256], in_=zb_ap[:, 0:256], func=AF.Square, accum_out=ss[:, 0:1]), reads=[zbb], writes=[ssb[0], cn_b[sl][0]])
                P.op("act", lambda e: e.activation(out=cn[sl][:, 256:384], in_=zb_ap[:, 256:384], func=AF.Square, accum_out=ss[:, 1:2]), reads=[zbb], writes=[ssb[1], cn_b[sl][1]])
                rs, rsb = stat(2)
                rstd_ops(ss, ssb, rs, rsb, 1, n=2)
                P.op("act", lambda e: e.activation(out=cn[sl][:, 0:256], in_=zb_ap[:, 0:256], func=AF.Copy, scale=rs[:, 0:1]), reads=[zbb] + rsb, writes=[cn_b[sl][0]])
                P.op("act", lambda e: e.activation(out=cn[sl][:, 256:384], in_=zb_ap[:, 256:384], func=AF.Copy, scale=rs[:, 1:2]), reads=[zbb] + rsb, writes=[cn_b[sl][1]])

            def post1(g, j, parts="abc"):
                sl = (g * 4 + j) % 2
                tb_ap, tbb = tbank()

                def f(e):
                    for i in range(8):
                        ins = e.transpose(tb_ap[:, i * 128:(i + 1) * 128], qkr[sl][:, i * 128:(i + 1) * 128], ident)
                    return ins
                P.op("pe", f, reads=qkr_b[sl], writes=[tbb])
                P.op("act", lambda e: e.activation(out=QTt[sl], in_=tb_ap[:, 0:512], func=AF.Copy), reads=[tbb], writes=[QTt_b[sl]])
                P.op("act", lambda e: e.activation(out=KTt[sl], in_=tb_ap[:, 512:1024], func=AF.Copy), reads=[tbb], writes=[KTt_b[sl]])
                if "b" in parts:
                    extra = []
                    P.op("dve", lambda e: e.tensor_tensor(out=qxs[0][:, :, j * 128:(j + 1) * 128], in0=v3(QTt[sl], 4), in1=v3(xif, 4), op=ALU.mult),
                         reads=[QTt_b[sl]] + extra, writes=[qxs_b[0][j]])
                    P.op("dve", lambda e: e.tensor_tensor(out=qxs[1][:, :, j * 128:(j + 1) * 128], in0=v3(QTt[sl], 4), in1=v3(xib, 4), op=ALU.mult),
                         reads=[QTt_b[sl]] + extra, writes=[qxs_b[1][j]])
                if "c" not in parts:
                    return
                tc_ap, tcb = tbank()

                def f2(e):
                    for i in range(3):
                        ins = e.transpose(tc_ap[:, i * 128:(i + 1) * 128], cn[sl][:, i * 128:(i + 1) * 128], ident)
                    return ins
                P.op("pe", f2, reads=cn_b[sl], writes=[tcb])
                P.op("act", lambda e: e.activation(out=cqT[:, :, j * 128:(j + 1) * 128], in_=v3(tc_ap[:, 0:256], 2), func=AF.Copy), reads=[tcb], writes=[cqT_b[j]])
                P.op("act", lambda e: e.activation(out=ckvT[:, j * 128:(j + 1) * 128], in_=tc_ap[:, 256:384], func=AF.Copy), reads=[tcb], writes=[ckvT_b[j]])

            def post2(g, j):
                sl = (g * 4 + j) % 2
                _, tok0, _ = glist[g]
                tglob = (tok0 // 128) + j
                r_ap, rbb = rbank()

                def f(e):
                    for h in range(4):
                        ins = e.matmul(r_ap[:, h * 128:(h + 1) * 128], KTt[sl][:, h * 128:(h + 1) * 128], QTt[sl][:, h * 128:(h + 1) * 128], start=True, stop=True)
                    return ins
                P.op("pe", f, reads=[KTt_b[sl], QTt_b[sl]], writes=[rbb])
                P.op("dve", lambda e: e.tensor_tensor(out=Pm[sl], in0=r_ap, in1=mask, op=ALU.mult), reads=[rbb], writes=[Pm_b[sl]])
                for d_ in range(2):
                    u_ap, ubb = rbank()

                    def fu(e, d_=d_, u_ap=u_ap):
                        for h in range(4):
                            ins = e.matmul(u_ap[:, h * 128:(h + 1) * 128], Kz[sl][:, d_, h * 128:(h + 1) * 128], vb[sl][:, h * 128:(h + 1) * 128], start=True, stop=True)
                        return ins
                    P.op("pe", fu, reads=[Kz_b[sl][d_], vb_b[sl]], writes=[ubb])
                    P.op("dve", lambda e, d_=d_, u_ap=u_ap: e.tensor_copy(out=Us[sl][:, d_, :], in_=u_ap), reads=[ubb], writes=[Us_b[sl][d_]])
                    P.dma((UF, UB)[d_][tglob], Us[sl][:, d_, :], reads=[Us_b[sl][d_]])

            def post3(g, j):
                sl = (g * 4 + j) % 2
                _, tok0, _ = glist[g]
                o_ap, obb = rbank()

                def f(e):
                    for h in range(4):
                        ins = e.matmul(o_ap[:, h * 128:(h + 1) * 128], Pm[sl][:, h * 128:(h + 1) * 128], vb[sl][:, h * 128:(h + 1) * 128], start=True, stop=True)
                    return ins
                P.op("pe", f, reads=[Pm_b[sl], vb_b[sl]], writes=[obb])
                P.op("act", lambda e: e.activation(out=OIs[sl], in_=o_ap, func=AF.Copy), reads=[obb], writes=[OIs_b[sl]])
                P.dma(OI[tok0 + j * 128:tok0 + (j + 1) * 128, :], OIs[sl], reads=[OIs_b[sl]])
                zb_ap, zbb = zbank()
                P.op("pe", lambda e: e.matmul(zb_ap, ckvT[:, j * 128:(j + 1) * 128], Wv, start=True, stop=True),
                     reads=[ckvT_b[j]], writes=[zbb])
                P.op("act", lambda e: e.activation(out=Vs[:, j, :], in_=zb_ap, func=AF.Copy), reads=[zbb], writes=[Vs_b[j]])

            def s3_qn(g, h):
                _, tok0, _ = glist[g]
                zb_ap, zbb = zbank()

                def f(e):
                    for kc in range(2):
                        ins = e.matmul(zb_ap, Wuq[:, kc, h * 192:h * 192 + 128], cqT[:, kc, :], start=(kc == 0), stop=(kc == 1))
                    return ins
                P.op("pe", f, reads=cqT_b, writes=[zbb])
                sl_ap, slb = slot()
                P.op("act", lambda e: e.activation(out=sl_ap, in_=zb_ap, func=AF.Copy), reads=[zbb], writes=[slb])
                P.dma(QTN[h, :, tok0:tok0 + 512], sl_ap, reads=[slb])

            def s3_qp(g, pr):
                _, tok0, _ = glist[g]
                za, zab = zbank()
                zr, zrb = zbank()

                def fa(e):
                    for kc in range(2):
                        ins = e.matmul(za, Wqpe[:, kc, pr, :], cqT[:, kc, :], start=(kc == 0), stop=(kc == 1))
                    return ins

                def fr(e):
                    for kc in range(2):
                        ins = e.matmul(zr, Wqpr[:, kc, pr, :], cqT[:, kc, :], start=(kc == 0), stop=(kc == 1))
                    return ins
                P.op("pe", fa, reads=cqT_b, writes=[zab])
                P.op("pe", fr, reads=cqT_b, writes=[zrb])
                tm, tmb = tabm[g % 2], tabm_b[g % 2]
                i0, i1 = nxt("mr", 2), nxt("mr", 2)
                P.op("dve", lambda e: e.tensor_tensor(out=mrt[i0], in0=za, in1=tm[0], op=ALU.mult), reads=[zab, tmb[0]], writes=[mrt_b[i0]])
                P.op("dve", lambda e: e.tensor_tensor(out=mrt[i1], in0=zr, in1=tm[1], op=ALU.mult), reads=[zrb, tmb[1]], writes=[mrt_b[i1]])
                P.op("pool", lambda e: e.tensor_tensor(out=QTPs[:, pr, :], in0=mrt[i0], in1=mrt[i1], op=ALU.add),
                     reads=[mrt_b[i0], mrt_b[i1]], writes=[QTPs_b[pr]])
                P.dma(QTP[pr, :, tok0:tok0 + 512], QTPs[:, pr, :], reads=[QTPs_b[pr]])

            def s3_kn(g, h):
                _, tok0, _ = glist[g]
                zb_ap, zbb = zbank()
                P.op("pe", lambda e: e.matmul(zb_ap, Wukv[:, h * 256:h * 256 + 128], ckvT, start=True, stop=True), reads=ckvT_b, writes=[zbb])
                sl_ap, slb = slot()
                P.op("act", lambda e: e.activation(out=sl_ap, in_=zb_ap, func=AF.Copy), reads=[zbb], writes=[slb])
                P.dma(KTN[h, :, tok0:tok0 + 512], sl_ap, reads=[slb])

            def s3_stores(g):
                _, tok0, _ = glist[g]
                P.dma(QXF[:, :, tok0:tok0 + 512].rearrange("h p t -> p h t"), qxs[0], reads=qxs_b[0])
                P.dma(QXB[:, :, tok0:tok0 + 512].rearrange("h p t -> p h t"), qxs[1], reads=qxs_b[1])
                P.dma(VV[tok0:tok0 + 512, :].rearrange("(j p) c -> p j c", p=128), Vs, reads=Vs_b)

            for j in range(4):
                s0_pre(0, j)
                s0_pe(0, j)
            for g in range(NG):
                if stop_after == "A0":
                    break
                items = [lambda g=g: fm_kpe(g)]
                for kind, n in (("mg", 4), ("rg", 4), ("gr", 8), ("gm", 8)):
                    for i in range(n):
                        items.append(lambda g=g, kind=kind, i=i: fm_gate(g, kind, i))
                def run_item(n_):
                    if g + 1 < NG and n_ == 0:
                        s0_load(g + 1, 0)
                        s0_load(g + 1, 1)
                    items[n_]()
                    if g + 1 < NG:
                        if n_ == 9:
                            s0_norm(g + 1, 0)
                            s0_load(g + 1, 2)
                        elif n_ == 13:
                            s0_norm(g + 1, 1)
                            s0_load(g + 1, 3)
                        elif n_ == 17:
                            s0_pe(g + 1, 0)
                        elif n_ == 19:
                            s0_norm(g + 1, 2)
                        elif n_ == 21:
                            s0_pe(g + 1, 1)
                        elif n_ == 23:
                            s0_norm(g + 1, 3)
                for n_ in range(21):
                    run_item(n_)
                tm_rope(g, 0, 0)
                tm_rope(g, 0, 1)
                run_item(21)
                run_item(22)
                tm_v(g, 0)
                tm_c(g, 0)
                run_item(23)
                run_item(24)
                for j in range(1, 4):
                    if g + 1 < NG and j in (1, 2):
                        s0_pe(g + 1, j + 1)
                    tm_rope(g, j, 0)
                    tm_rope(g, j, 1)
                    post1(g, j - 1)
                    if j >= 2:
                        post3(g, j - 2)
                    tm_v(g, j)
                    tm_c(g, j)
                    post2(g, j - 1)
                post1(g, 3)
                post3(g, 2)
                post2(g, 3)
                for h in range(4):
                    s3_qn(g, h)
                post3(g, 3)
                for pr in range(2):
                    s3_qp(g, pr)
                for h in range(4):
                    s3_kn(g, h)
                s3_stores(g)
            P.barrier()
            if stop_after is not None and stop_after.startswith("A"):
                break

            A.release(base_mark)
            gng = A.alloc(F32, 4)
            gng_b = Buf()
            P.dma(gng, W_gn[L].rearrange("(h p) -> p h", p=128), writes=[gng_b], slow=True)
            P.op("dve", lambda e: e.tensor_scalar(out=gng, in0=gng, scalar1=float(np.sqrt(128.0)), scalar2=None, op0=ALU.mult),
                 reads=[gng_b], writes=[gng_b])
            decf = A.alloc(F32, 512)
            decb = A.alloc(F32, 512)
            P.dma(decf, Cd["rows"][2].partition_broadcast(128))
            P.dma(decb, Cd["rows"][3].partition_broadcast(128))
            P.barrier()
            Sst = [[A.alloc(F32, 512) for _ in range(2)] for _ in range(2)]
            Sst_b = [bufs(2), bufs(2)]
            NU = 4
            Ust = [[A.alloc(F32, 512) for _ in range(NU)] for _ in range(2)]
            Ust_b = [bufs(NU), bufs(NU)]
            Sn = [[A.alloc(BF16, 512) for _ in range(NU)] for _ in range(2)]
            Sn_b = [bufs(NU), bufs(NU)]
            SFd_b = bufs(NTt)
            SBd_b = bufs(NTt)

            def scan_load(d_, t, i):
                k = i % NU
                P.dma(Ust[d_][k], (UF, UB)[d_][t], writes=[Ust_b[d_][k]])

            def scan_step(d_, t, i, update):
                eng = ("dve", "pool")[d_]
                dec = (decf, decb)[d_]
                k = i % NU
                So, So_b = Sst[d_][i % 2], Sst_b[d_][i % 2]
                Sn_, Sn_b_ = Sst[d_][(i + 1) % 2], Sst_b[d_][(i + 1) % 2]
                U_ap, U_b = Ust[d_][k], Ust_b[d_][k]
                N_ap, N_b = Sn[d_][k], Sn_b[d_][k]
                P.op(eng, lambda e: e.tensor_copy(out=N_ap, in_=So), reads=[So_b], writes=[N_b])
                P.dma((SF, SB)[d_][t], N_ap, reads=[N_b], writes=[(SFd_b, SBd_b)[d_][t]])
                if update:
                    P.op(eng, lambda e: e.tensor_tensor(out=Sn_, in0=So, in1=dec, op=ALU.mult), reads=[So_b], writes=[Sn_b_])
                    P.op(eng, lambda e: e.tensor_tensor(out=Sn_, in0=Sn_, in1=U_ap, op=ALU.add), reads=[Sn_b_, U_b], writes=[Sn_b_])

            def scan_init(d_):
                eng = ("dve", "pool")[d_]
                S_ap, S_b = Sst[d_][0], Sst_b[d_][0]
                P.op(eng, lambda e: e.memset(S_ap, 0.0), writes=[S_b])

            def scan_list(si):
                nt_ = seqs[si] // 128
                t0_ = seq_off[si] // 128
                PF = NU - 1

                def tf(i):
                    return t0_ + i

                def tb(i):
                    return t0_ + nt_ - 1 - i

                def first():
                    scan_init(0)
                    scan_init(1)
                    for i in range(min(PF, nt_ - 1)):
                        scan_load(0, tf(i), i)
                        scan_load(1, tb(i), i)
                q = [first]
                for i in range(nt_):
                    def step(i=i):
                        if i + PF < nt_ - 1:
                            scan_load(0, tf(i + PF), i + PF)
                            scan_load(1, tb(i + PF), i + PF)
                        scan_step(0, tf(i), i, i < nt_ - 1)
                        scan_step(1, tb(i), i, i < nt_ - 1)
                    q.append(step)
                return q
            scan_q = []

            KTs = A.alloc(BF16, 4, SMAX)
            KPs = A.alloc(BF16, 2, SMAX)
            Vsq = A.alloc(BF16, SMAX // 128, 512)
            KT_b = bufs(4)
            KP_b = Buf()
            V_b = Buf()
            QN = [A.alloc(BF16, 4, 512) for _ in range(2)]
            QN_b = bufs(2)
            QP = [A.alloc(BF16, 2, 512) for _ in range(2)]
            QP_b = bufs(2)
            MG = [A.alloc(BF16, 4, 512) for _ in range(2)]
            MG_b = bufs(2)
            NPT = 6
            PT = [A.alloc(BF16, 512) for _ in range(NPT)]
            PT_b = bufs(NPT)
            cpend = [None]
            rec = A.alloc(F32, 512)
            rec_b = Buf()
            an = A.alloc(F32, 512)
            an_b = Buf()
            ags = [A.alloc(BF16, 512) for _ in range(2)]
            ags_b = bufs(2)
            _qxf, _qxb, _sfg, _sbg = A.alloc(BF16, 4, 512), A.alloc(BF16, 4, 512), A.alloc(BF16, 4, 512), A.alloc(BF16, 4, 512)
            QXf, QXb_, SFg, SBg = [_qxf, _qxf], [_qxb, _qxb], [_sfg, _sfg], [_sbg, _sbg]
            _qxbuf, _sgbuf = bufs(2), bufs(2)
            QX_b, SG_b = [_qxbuf, _qxbuf], [_sgbuf, _sgbuf]
            _oig, _rgg = A.alloc(F32, 4, 512), A.alloc(BF16, 4, 512)
            OIg, RGg = [_oig, _oig], [_rgg, _rgg]
            _oib, _rgb = Buf(), Buf()
            OIg_b, RGg_b = [_oib, _oib], [_rgb, _rgb]
            of = [A.alloc(F32, 512) for _ in range(2)]
            of_b = bufs(2)
            dd = [A.alloc(F32, 512) for _ in range(2)]
            dd_b = bufs(2)
            sq = A.alloc(F32, 512)
            sq_b = Buf()
            on = [A.alloc(BF16, 512) for _ in range(4)]
            on_b = bufs(4)
            tce = A.alloc(BF16, 512)
            tce_b = Buf()
            rgs = A.alloc(BF16, 4, 512)
            rgs_b = bufs(4)
            Sbk = bufs(3)
            Abk = bufs(2)
            Zbk = bufs(2)
            RObk = Buf()
            TCbk = RObk
            cc = {"S": 0, "AZ": 0, "PT": 0, "ag": 0, "t": 0, "it": 0, "rate": 1}
            scale = float(192.0 ** -0.5)

            def c_ret1(gsl, j):
                ro = bank(7)
                t = j % 2

                def f(e):
                    for h in range(4):
                        e.matmul(ro[:, h * 128:(h + 1) * 128], QXf[gsl][:, h, j * 128:(j + 1) * 128], SFg[gsl][:, j, h * 128:(h + 1) * 128], start=True, stop=False)
                        ins = e.matmul(ro[:, h * 128:(h + 1) * 128], QXb_[gsl][:, h, j * 128:(j + 1) * 128], SBg[gsl][:, j, h * 128:(h + 1) * 128], start=False, stop=True)
                    return ins
                P.op("pe", f, reads=QX_b[gsl] + SG_b[gsl], writes=[RObk])
                P.op("dve", lambda e: e.tensor_tensor(out=of[t], in0=ro, in1=OIg[gsl][:, j, :], op=ALU.add), reads=[RObk, OIg_b[gsl]], writes=[of_b[t]])
                s1, s1b = stat(4)
                P.op("dve", lambda e: e.tensor_reduce(out=s1, in_=v3(of[t], 4), axis=AX.X, op=ALU.add), reads=[of_b[t]], writes=s1b)
                mu, mub = stat(4)
                P.op("dve", lambda e: e.tensor_scalar(out=mu, in0=s1, scalar1=1.0 / 128.0, scalar2=None, op0=ALU.mult), reads=s1b, writes=mub)
                P.op("dve", lambda e: e.tensor_tensor(out=v3(dd[t], 4), in0=v3(of[t], 4), in1=mu.unsqueeze(2).to_broadcast([128, 4, 128]), op=ALU.subtract),
                     reads=[of_b[t]] + mub, writes=[dd_b[t]])
                P.op("dve", lambda e: e.tensor_tensor(out=sq, in0=dd[t], in1=dd[t], op=ALU.mult), reads=[dd_b[t]], writes=[sq_b])
                s2, s2b = stat(4)
                P.op("dve", lambda e: e.tensor_reduce(out=s2, in_=v3(sq, 4), axis=AX.X, op=ALU.add), reads=[sq_b], writes=s2b)
                rs4, rs4b = stat(4)
                tmp4, tmp4b = stat(4)
                P.op("pool", lambda e: e.tensor_tensor(out=tmp4, in0=s2, in1=cst[:, 6:7].to_broadcast([128, 4]), op=ALU.add), reads=s2b, writes=tmp4b)
                P.op("pool", lambda e: e.tensor_tensor(out=rs4, in0=tmp4, in1=cst[:, 3:4].to_broadcast([128, 4]), op=ALU.pow), reads=tmp4b, writes=rs4b)
                P.op("pool", lambda e: e.tensor_tensor(out=v3(on[j], 4), in0=v3(dd[t], 4), in1=rs4.unsqueeze(2).to_broadcast([128, 4, 128]), op=ALU.mult),
                     reads=[dd_b[t]] + rs4b, writes=[on_b[j]])

            def c_ret2(gsl, j):
                tcb = bankb(7)

                def f2(e):
                    for h in range(4):
                        ins = e.transpose(tcb[:, h * 128:(h + 1) * 128], on[j][:, h * 128:(h + 1) * 128], ident)
                    return ins
                P.op("pe", f2, reads=[on_b[j]], writes=[TCbk])
                for h in range(4):
                    P.op("act", lambda e, h=h: e.activation(out=tce[:, h * 128:(h + 1) * 128], in_=tcb[:, h * 128:(h + 1) * 128], func=AF.Copy, scale=gng[:, h:h + 1]),
                         reads=[TCbk, gng_b], writes=[tce_b])
                P.op("pool", lambda e: e.tensor_tensor(out=rgs[:, :, j * 128:(j + 1) * 128], in0=v3(tce, 4), in1=RGg[gsl][:, :, j * 128:(j + 1) * 128], op=ALU.mult),
                     reads=[tce_b, RGg_b[gsl]], writes=[rgs_b[j]])

            def c_scores(gsl, h, kt, hp):
                hv = h % 2
                pr = h // 2
                sb_i = cc["S"] % 3
                cc["S"] += 1
                s_ap = bank(sb_i)

                def f(e):
                    e.matmul(s_ap, KTs[:, h, kt * 128:(kt + 1) * 128], QN[gsl][:, h, :], start=True, stop=False)
                    return e.matmul(s_ap, KPs[:, hv, kt * 128:(kt + 1) * 128], QP[gsl][:, pr, :], start=False, stop=True)
                P.op("pe", f, reads=[KT_b[h], KP_b, QN_b[gsl], QP_b[gsl]], writes=[Sbk[sb_i]])
                pi = cc["PT"] % NPT
                cc["PT"] += 1
                P.op("act", lambda e: e.activation(out=PT[pi], in_=s_ap, func=AF.Exp, scale=scale), reads=[Sbk[sb_i]], writes=[PT_b[pi]])
                return pi

            def c_pv(h, kt, nt, pi, az):
                a_ap, z_ap = bank(3 + az), bank(5 + az)
                st_, sp__ = (kt == 0), (kt == nt - 1)

                def f(e):
                    e.matmul(a_ap, Vsq[:, kt, h * 128:(h + 1) * 128], PT[pi], start=st_, stop=sp__)
                    return e.matmul(z_ap, ones, PT[pi], start=st_, stop=sp__)
                P.op("pe", f, reads=[V_b, PT_b[pi]], writes=[Abk[az], Zbk[az]])

            def c_head(gsl, h, nt, tok0, extras, pending):
                az = cc["AZ"] % 2
                cc["AZ"] += 1
                hp = az
                a_ap, z_ap = bank(3 + az), bank(5 + az)
                step = max(1, nt // 2)
                la = min(2, nt - 1)
                pis = [c_scores(gsl, h, k, hp) for k in range(la + 1)]
                for kt in range(nt):
                    if kt + la + 1 < nt:
                        pis.append(c_scores(gsl, h, kt + la + 1, hp))
                    c_pv(h, kt, nt, pis[kt], az)
                    if pending is not None and kt == min(1, nt - 1):
                        pending()
                        pending = None
                    cc["it"] += 1
                    if scan_q and cc["it"] % cc["rate"] == 0:
                        scan_q.pop(0)()
                    if extras and (kt % step) == step - 1:
                        extras.pop(0)()

                def epilogue():
                    P.op("dve", lambda e: e.reciprocal(out=rec, in_=z_ap), reads=[Zbk[az]], writes=[rec_b])
                    P.op("dve", lambda e: e.tensor_tensor(out=an, in0=a_ap, in1=rec, op=ALU.mult), reads=[Abk[az], rec_b], writes=[an_b])
                    ai = cc["ag"] % 2
                    cc["ag"] += 1
                    P.op("pool", lambda e: e.tensor_tensor(out=ags[ai], in0=an, in1=MG[gsl][:, h, :], op=ALU.mult), reads=[an_b, MG_b[gsl]], writes=[ags_b[ai]])
                    P.dma(AG[h, :, tok0:tok0 + 512], ags[ai], reads=[ags_b[ai]])
                return epilogue

            def c_gloads(gsl, tok0):
                P.dma(QN[gsl], QTN[:, :, tok0:tok0 + 512].rearrange("h p t -> p h t"), writes=[QN_b[gsl]])
                P.dma(QP[gsl], QTP[:, :, tok0:tok0 + 512].rearrange("q p t -> p q t"), writes=[QP_b[gsl]])
                P.dma(MG[gsl], MGT[:, :, tok0:tok0 + 512].rearrange("h p t -> p h t"), writes=[MG_b[gsl]])

            def c_gloads_ret(tok0):
                gsl = 0
                P.dma(QXf[gsl], QXF[:, :, tok0:tok0 + 512].rearrange("h p t -> p h t"), writes=[QX_b[gsl][0]])
                P.dma(QXb_[gsl], QXB[:, :, tok0:tok0 + 512].rearrange("h p t -> p h t"), writes=[QX_b[gsl][1]])
                tg0 = tok0 // 128
                P.dma(SFg[gsl], SF[tg0:tg0 + 4].rearrange("j p c -> p j c"), reads=SFd_b[tg0:tg0 + 4], writes=[SG_b[gsl][0]])
                P.dma(SBg[gsl], SB[tg0:tg0 + 4].rearrange("j p c -> p j c"), reads=SBd_b[tg0:tg0 + 4], writes=[SG_b[gsl][1]])
                P.dma(OIg[gsl], OI[tok0:tok0 + 512, :].rearrange("(j p) c -> p j c", p=128), writes=[OIg_b[gsl]])
                P.dma(RGg[gsl], RGT[:, :, tok0:tok0 + 512].rearrange("h p t -> p h t"), writes=[RGg_b[gsl]])

            def c_group(gidx, tok0, nt, next_tok0):
                gsl = gidx % 2
                extras = []
                if next_tok0 is not None:
                    extras.append(lambda: c_gloads(1 - gsl, next_tok0))
                extras += [lambda: c_ret1(gsl, 0),
                           lambda: c_ret1(gsl, 1),
                           lambda: (c_ret2(gsl, 0), c_ret1(gsl, 2)),
                           lambda: (c_ret2(gsl, 1), c_ret1(gsl, 3)),
                           lambda: c_ret2(gsl, 2),
                           lambda: c_ret2(gsl, 3)]
                if next_tok0 is not None:
                    extras.append(lambda: c_gloads_ret(next_tok0))
                for h in range(4):
                    cpend[0] = c_head(gsl, h, nt, tok0, extras, cpend[0])
                while extras:
                    extras.pop(0)()
                P.dma(RETG[:, :, tok0:tok0 + 512].rearrange("h p t -> p h t"), rgs, reads=rgs_b)

            cgl = [(si, seq_off[si] + gi * 512) for si, s in enumerate(seqs) for gi in range(s // 512)]
            for f_ in scan_list(0):
                f_()
            c_gloads(0, cgl[0][1])
            c_gloads_ret(cgl[0][1])
            for gidx, (si, tok0) in enumerate(cgl):
                s = seqs[si]
                s0 = seq_off[si]
                nt = s // 128
                if tok0 == s0:
                    P.dma(KTs[:, 0, 0:s], KTN[0, :, s0:s0 + s], writes=[KT_b[0]])
                    P.dma(KPs[:, :, 0:s], KTP[:, :, s0:s0 + s].rearrange("v p t -> p v t"), writes=[KP_b])
                    for j0 in range(0, nt, 8):
                        j1 = min(nt, j0 + 8)
                        P.dma(Vsq[:, j0:j1, :], VV[s0 + j0 * 128:s0 + j1 * 128, :].rearrange("(j p) c -> p j c", p=128), writes=[V_b])
                    for h in range(1, 4):
                        P.dma(KTs[:, h, 0:s], KTN[h, :, s0:s0 + s], writes=[KT_b[h]])
                    assert not scan_q
                    if si + 1 < len(seqs):
                        scan_q.extend(scan_list(si + 1))
                        n_it = max(1, (s // 512 - 1)) * 4 * nt
                        cc["rate"] = max(1, n_it // (len(scan_q) + 1))
                last_of_seq = (tok0 + 512 == s0 + s)
                if last_of_seq:
                    while scan_q:
                        scan_q.pop(0)()
                c_group(gidx, tok0, nt, cgl[gidx + 1][1] if gidx + 1 < len(cgl) else None)
            cpend[0]()
            cpend[0] = None
            P.barrier()
            if stop_after == "C":
                break

            A.release(base_mark)
            Wr = A.alloc(BF16, 4, D)
            Wm = A.alloc(BF16, 4, D)
            Wo = A.alloc(BF16, KC, D)
            gfin = A.alloc(F32, D)
            wD_mark = A.mark()
            stgD = [A.alloc(F32, 1024), A.alloc(F32, 1024)]
            stgD_b = bufs(2)

            def d_wload(src, dst, kc, k, use_act):
                P.dma(stgD[k], src[L, kc * 128:(kc + 1) * 128, :], writes=[stgD_b[k]])
                if use_act:
                    P.op("act", lambda e: e.activation(out=dst[:, kc, :], in_=stgD[k], func=AF.Copy), reads=[stgD_b[k]])
                else:
                    P.op("dve", lambda e: e.tensor_copy(out=dst[:, kc, :], in_=stgD[k]), reads=[stgD_b[k]])
            wi = 0
            for (src, dst, n) in ((W_brr, Wr, 4), (W_brm, Wm, 4), (W_out, Wo, 8)):
                for kc in range(n):
                    d_wload(src, dst, kc, wi % 2, wi % 2 == 0)
                    wi += 1
            if last:
                gf_b = Buf()
                P.dma(gfin, W_fin.partition_broadcast(128), writes=[gf_b])
                P.op("dve", lambda e: e.tensor_scalar(out=gfin, in0=gfin, scalar1=32.0, scalar2=None, op0=ALU.mult), reads=[gf_b], writes=[gf_b])
            P.barrier()
            A.release(wD_mark)
            RG2 = [A.alloc(BF16, 4, 512) for _ in range(2)]
            AG2 = [A.alloc(BF16, 4, 512) for _ in range(2)]
            GR2 = [A.alloc(BF16, 8, 512) for _ in range(2)]
            GM2 = [A.alloc(BF16, 8, 512) for _ in range(2)]
            in_b = [bufs(4), bufs(4)]
            mer = [A.alloc(BF16, 8, 512) for _ in range(2)]
            mer_b = [bufs(8), bufs(8)]
            NX = 8
            xr = [A.alloc(F32, D) for _ in range(NX)]
            xr_b = bufs(NX)
            t1 = [A.alloc(F32, 512) for _ in range(2)]
            t1_b = bufs(2)
            t2 = [A.alloc(F32, 512) for _ in range(2)]
            t2_b = bufs(2)
            jk = A.alloc(BF16, D)
            Pbk = bufs(8)
            cd = {"p": 0, "t": 0, "x": 0}
            Xres = Xin
            Xout = Y if last else XMID

            def pbank():
                i = cd["p"] % 8
                cd["p"] += 1
                return bank(i), Pbk[i]

            glistD = [(seq_off[si] + gi * 512) for si, s in enumerate(seqs) for gi in range(s // 512)]

            def d_loads(g):
                tok0 = glistD[g]
                k = g % 2
                P.dma(RG2[k], RETG[:, :, tok0:tok0 + 512].rearrange("h p t -> p h t"), writes=[in_b[k][0]])
                P.dma(AG2[k], AG[:, :, tok0:tok0 + 512].rearrange("h p t -> p h t"), writes=[in_b[k][1]])
                P.dma(GR2[k], GRT[:, :, tok0:tok0 + 512].rearrange("h p t -> p h t"), writes=[in_b[k][2]])
                P.dma(GM2[k], GMT[:, :, tok0:tok0 + 512].rearrange("h p t -> p h t"), writes=[in_b[k][3]])

            def d_merge(k, dc):
                pa, pab = pbank()
                pm_, pmb = pbank()

                def fa(e):
                    for h in range(4):
                        ins = e.matmul(pa, Wr[:, h, dc * 128:(dc + 1) * 128], RG2[k][:, h, :], start=(h == 0), stop=(h == 3))
                    return ins

                def fm(e):
                    for h in range(4):
                        ins = e.matmul(pm_, Wm[:, h, dc * 128:(dc + 1) * 128], AG2[k][:, h, :], start=(h == 0), stop=(h == 3))
                    return ins
                P.op("pe", fa, reads=[in_b[k][0]], writes=[pab])
                P.op("pe", fm, reads=[in_b[k][1]], writes=[pmb])
                ti = cd["t"] % 2
                cd["t"] += 1
                P.op("dve", lambda e: e.tensor_tensor(out=t1[ti], in0=pa, in1=GR2[k][:, dc, :], op=ALU.mult), reads=[pab, in_b[k][2]], writes=[t1_b[ti]])
                P.op("dve", lambda e: e.tensor_tensor(out=t2[ti], in0=pm_, in1=GM2[k][:, dc, :], op=ALU.mult), reads=[pmb, in_b[k][3]], writes=[t2_b[ti]])
                P.op("pool", lambda e: e.tensor_tensor(out=mer[k][:, dc, :], in0=t1[ti], in1=t2[ti], op=ALU.add), reads=[t1_b[ti], t2_b[ti]], writes=[mer_b[k][dc]])

            def d_out_half(k, j, hf, xi):
                po, pob = pbank()

                def fo(e):
                    for dc in range(8):
                        ins = e.matmul(po, mer[k][:, dc, j * 128:(j + 1) * 128], Wo[:, dc, hf * 512:(hf + 1) * 512], start=(dc == 0), stop=(dc == 7))
                    return ins
                P.op("pe", fo, reads=mer_b[k], writes=[pob])
                P.op("dve", lambda e: e.tensor_tensor(out=xr[xi][:, hf * 512:(hf + 1) * 512], in0=po, in1=xr[xi][:, hf * 512:(hf + 1) * 512], op=ALU.add),
                     reads=[pob, xr_b[xi]], writes=[xr_b[xi]])

            def d_xload(g):
                for j in range(4):
                    xi = (g * 4 + j) % NX
                    r0 = glistD[g] + j * 128
                    P.dma(xr[xi], Xres[r0:r0 + 128, :], writes=[xr_b[xi]])

            def d_tile(g, k, tok0, j):
                xi = (g * 4 + j) % NX
                r0 = tok0 + j * 128
                for hf in range(2):
                    d_out_half(k, j, hf, xi)
                if last:
                    ss, ssb = stat()
                    P.op("act", lambda e: e.activation(out=jk, in_=xr[xi], func=AF.Square, accum_out=ss), reads=[xr_b[xi]], writes=ssb)
                    rs, rsb = stat()
                    rstd_ops(ss, ssb, rs, rsb, 0)
                    P.op("dve", lambda e: e.scalar_tensor_tensor(out=xr[xi], in0=xr[xi], scalar=rs, in1=gfin, op0=ALU.mult, op1=ALU.mult),
                         reads=[xr_b[xi]] + rsb, writes=[xr_b[xi]])
                P.dma(Xout[r0:r0 + 128, :], xr[xi], reads=[xr_b[xi]])

            ND = len(glistD)
            d_loads(0)
            d_xload(0)
            for dc in range(8):
                d_merge(0, dc)
            for g in range(ND):
                if g + 1 < ND:
                    d_loads(g + 1)
                    d_xload(g + 1)
                for j in range(4):
                    d_tile(g, g % 2, glistD[g], j)
                    if g + 1 < ND:
                        d_merge((g + 1) % 2, 2 * j)
                        d_merge((g + 1) % 2, 2 * j + 1)
            P.barrier()

        P.barrier()
        P.emit()
        build_program.peak = A.peak
        build_program.P = P
    return nc


_CACHE = {}


def _get_program(seqs):
    key = tuple(seqs)
    if key not in _CACHE:
        _CACHE[key] = build_program(seqs)
    return _CACHE[key]


def kernel(x_prompt, x_sample, norm_g, w_in, ret_gn_g, q_norm_g, kv_norm_g, w_uq, w_ukv, w_br_ret, w_br_mla, w_out, final_norm_g):
    x_prompt = np.asarray(x_prompt, dtype=np.float32)
    x_sample = np.asarray(x_sample, dtype=np.float32)
    nb_p, s_p, _ = x_prompt.shape
    nb_s, s_s, _ = x_sample.shape
    pp = nb_p // N_CORES
    sp_ = nb_s // N_CORES
    seqs = tuple([s_p] * pp + [s_s] * sp_)
    nc = _get_program(seqs)
    consts = host_consts(max(seqs))
    wts = {
        "norm_g": np.ascontiguousarray(norm_g, dtype=np.float32), "w_in": np.ascontiguousarray(w_in, dtype=np.float32),
        "ret_gn_g": np.ascontiguousarray(ret_gn_g, dtype=np.float32), "q_norm_g": np.ascontiguousarray(q_norm_g, dtype=np.float32),
        "kv_norm_g": np.ascontiguousarray(kv_norm_g, dtype=np.float32), "w_uq": np.ascontiguousarray(w_uq, dtype=np.float32),
        "w_ukv": np.ascontiguousarray(w_ukv, dtype=np.float32), "w_br_ret": np.ascontiguousarray(w_br_ret, dtype=np.float32),
        "w_br_mla": np.ascontiguousarray(w_br_mla, dtype=np.float32), "w_out": np.ascontiguousarray(w_out, dtype=np.float32),
        "final_norm_g": np.ascontiguousarray(final_norm_g, dtype=np.float32),
    }
    in_maps = []
    for c in range(N_CORES):
        xc = np.concatenate([x_prompt[c * pp:(c + 1) * pp].reshape(pp * s_p, D), x_sample[c * sp_:(c + 1) * sp_].reshape(sp_ * s_s, D)], axis=0)
        m = {"x": xc}
        m.update(wts)
        m.update(consts)
        in_maps.append(m)
    res = run_bass_kernel_spmd(nc, in_maps, core_ids=list(range(N_CORES)))
    y_p = np.empty_like(x_prompt)
    y_s = np.empty_like(x_sample)
    for c in range(N_CORES):
        yc = np.asarray(res.results[c]["y"], dtype=np.float32)
        y_p[c * pp:(c + 1) * pp] = yc[:pp * s_p].reshape(pp, s_p, D)
        y_s[c * sp_:(c + 1) * sp_] = yc[pp * s_p:].reshape(sp_, s_s, D)
    return (y_p, y_s)
```
